# Optimizing a Trainium2 kernel written in Bass

```python
import jax, jax.numpy as jnp
from jax import lax
import numpy as np

D_MODEL = 1024
BATCH = 16
SEQ = 2048
DEPTH = 2

N_EVEN = (DEPTH + 1) // 2
N_ODD = DEPTH // 2
MIX_HALF = D_MODEL // 2

HGRN_HEADS = 4
HGRN_DK = 128
HGRN_DV = MIX_HALF // HGRN_HEADS
HGRN_CHUNK = 64
HGRN_K = HGRN_HEADS * HGRN_DK
HGRN_V = HGRN_HEADS * HGRN_DV
SGU_GROUPS = 4
SGU_CH = MIX_HALF // SGU_GROUPS
SGU_CHUNK = 128
CONV_CH = MIX_HALF
CONV_WIDTH = 31
MLA_HEADS = 4
MLA_NOPE = 128
MLA_ROPE = 64
MLA_V = 128
MLA_Q_RANK = 384
MLA_KV_RANK = 256
ATTN_BLOCK = 128
ROPE_THETA = 10000.0
D_FF = -(-8 * D_MODEL // (3 * 256)) * 256
EPS = 1e-6

IN_EVEN = 2 * HGRN_K + 2 * HGRN_V + 2 * MIX_HALF
IN_ODD = 2 * CONV_CH + MLA_Q_RANK + MLA_KV_RANK + MLA_ROPE

kernel_name = "hgrn2_gmlp_conformer_mla_hybrid"

F32 = jnp.float32


def rmsnorm(x, w):
    xf = x.astype(F32)
    y = xf * lax.rsqrt(jnp.mean(xf * xf, axis=-1, keepdims=True) + EPS)
    return (y * w.astype(F32)).astype(x.dtype)


def layernorm(x, g, b):
    xf = x.astype(F32)
    mu = jnp.mean(xf, axis=-1, keepdims=True)
    var = jnp.mean(jnp.square(xf - mu), axis=-1, keepdims=True)
    y = (xf - mu) * lax.rsqrt(var + EPS)
    return (y * g.astype(F32) + b.astype(F32)).astype(x.dtype)


def hgrn2(q, f_pre, i, g, lb, gnorm_w):
    B, T, _ = q.shape
    nc = T // HGRN_CHUNK
    lb = lb.astype(F32)
    f = lb + (1.0 - lb) * jax.nn.sigmoid(f_pre.astype(F32))
    log_f = jnp.log(f)
    k = 1.0 - f

    def to_chunks(a, d):
        return a.astype(F32).reshape(B, nc, HGRN_CHUNK, HGRN_HEADS, d).transpose(1, 0, 3, 2, 4)

    qc, kc, lfc = to_chunks(q, HGRN_DK), to_chunks(k, HGRN_DK), to_chunks(log_f, HGRN_DK)
    ic = to_chunks(i, HGRN_DV)
    causal = jnp.tril(jnp.ones((HGRN_CHUNK, HGRN_CHUNK), bool))[:, :, None]

    def step(S, inp):
        qb, kb, lfb, ib = inp
        b = jnp.cumsum(lfb, axis=2)
        o_inter = jnp.einsum('bhtd,bhdv->bhtv', qb * jnp.exp(b), S)
        diff = b[:, :, :, None, :] - b[:, :, None, :, :]
        decay = jnp.exp(jnp.where(causal, diff, -jnp.inf))
        scores = jnp.einsum('bhtd,bhtsd,bhsd->bhts', qb, decay, kb)
        o_intra = jnp.einsum('bhts,bhsv->bhtv', scores, ib)
        b_last = b[:, :, -1:, :]
        S = (jnp.exp(b_last[:, :, 0, :, None]) * S
             + jnp.einsum('bhsd,bhsv->bhdv', kb * jnp.exp(b_last - b), ib))
        return S, o_inter + o_intra

    S0 = jnp.zeros((B, HGRN_HEADS, HGRN_DK, HGRN_DV), F32)
    _, o = lax.scan(step, S0, (qc, kc, lfc, ic))
    o = o.transpose(1, 0, 3, 2, 4).reshape(B, T, HGRN_HEADS, HGRN_DV)
    o = rmsnorm(o, gnorm_w.reshape(HGRN_HEADS, HGRN_DV)).reshape(B, T, HGRN_V)
    return (o * jax.nn.silu(g.astype(F32))).astype(g.dtype)


def spatial_gating(u, v, ln_g, ln_b, w_s, b_s):
    B, T, _ = u.shape
    nc = T // SGU_CHUNK
    vn = layernorm(v.reshape(B, T, SGU_GROUPS, SGU_CH),
                   ln_g.reshape(SGU_GROUPS, SGU_CH), ln_b.reshape(SGU_GROUPS, SGU_CH))
    vn = vn.reshape(B, nc, SGU_CHUNK, SGU_GROUPS, SGU_CH)
    w = w_s * jnp.tril(jnp.ones((SGU_CHUNK, SGU_CHUNK), w_s.dtype))[None]
    z = jnp.einsum('gts,bnsgc->bntgc', w, vn) + b_s.T[:, :, None]
    return u * z.reshape(B, T, SGU_GROUPS * SGU_CH)


def conformer_conv(h_glu, conv_w, conv_b, ln_g, ln_b):
    a, gate = jnp.split(h_glu, 2, axis=-1)
    h = a * jax.nn.sigmoid(gate)
    h = lax.conv_general_dilated(h, conv_w[:, None, :].astype(h.dtype), window_strides=(1,),
                                 padding=[(CONV_WIDTH - 1, 0)],
                                 dimension_numbers=('NWC', 'WIO', 'NWC'),
                                 feature_group_count=CONV_CH) + conv_b
    return jax.nn.silu(layernorm(h, ln_g, ln_b))


def rope_tables(positions):
    inv = 1.0 / (ROPE_THETA ** (jnp.arange(0, MLA_ROPE, 2, dtype=F32) / MLA_ROPE))
    ang = positions.astype(F32)[..., None] * inv
    return jnp.cos(ang), jnp.sin(ang)


def apply_rope(x, cos, sin):
    x1, x2 = jnp.split(x.astype(F32), 2, axis=-1)
    return jnp.concatenate([x1 * cos - x2 * sin, x1 * sin + x2 * cos], axis=-1).astype(x.dtype)


def causal_mla_attention(q_nope, q_rope, k_nope, k_rope, v):
    T = q_nope.shape[1]
    scale = (MLA_NOPE + MLA_ROPE) ** -0.5
    outs = []
    for start in range(0, T, ATTN_BLOCK):
        end = start + ATTN_BLOCK
        s = (jnp.einsum('bqhd,bkhd->bhqk', q_nope[:, start:end], k_nope[:, :end],
                        preferred_element_type=F32)
             + jnp.einsum('bqhr,bkr->bhqk', q_rope[:, start:end], k_rope[:, :end],
                          preferred_element_type=F32)) * scale
        mask = (start + jnp.arange(ATTN_BLOCK))[:, None] >= jnp.arange(end)[None, :]
        p = jax.nn.softmax(jnp.where(mask, s, -jnp.inf), axis=-1)
        outs.append(jnp.einsum('bhqk,bkhv->bqhv', p.astype(v.dtype), v[:, :end]))
    return jnp.concatenate(outs, axis=1)


def mla(c_q, c_kv, k_rope_pre, positions, q_norm, w_uq, kv_norm, w_ukv):
    B, T, _ = c_q.shape
    q = (rmsnorm(c_q, q_norm) @ w_uq).reshape(B, T, MLA_HEADS, MLA_NOPE + MLA_ROPE)
    q_nope, q_rope = q[..., :MLA_NOPE], q[..., MLA_NOPE:]
    kv = (rmsnorm(c_kv, kv_norm) @ w_ukv).reshape(B, T, MLA_HEADS, MLA_NOPE + MLA_V)
    k_nope, v = kv[..., :MLA_NOPE], kv[..., MLA_NOPE:]
    cos, sin = rope_tables(positions)
    q_rope = apply_rope(q_rope, cos[:, :, None, :], sin[:, :, None, :])
    k_rope = apply_rope(k_rope_pre, cos, sin)
    o = causal_mla_attention(q_nope, q_rope, k_nope, k_rope, v)
    return o.reshape(B, T, MLA_HEADS * MLA_V)


def swiglu(h, w_gate, w_up, w_down):
    return (jax.nn.silu(h @ w_gate) * (h @ w_up)) @ w_down


def setup_inputs(seed: int = 0) -> dict:
    key = jax.random.key(seed)
    ks = iter(jax.random.split(key, 32))

    def dense(shape, fan_in):
        return jax.random.normal(next(ks), shape, F32) * fan_in ** -0.5

    def gain(shape):
        return 1.0 + 0.02 * jax.random.normal(next(ks), shape, F32)

    def bias(shape):
        return 0.02 * jax.random.normal(next(ks), shape, F32)

    x = jax.random.normal(next(ks), (BATCH, SEQ, D_MODEL), F32)
    offsets = jax.random.randint(next(ks), (BATCH, 1), 0, 4096, dtype=jnp.int32)
    positions = offsets + jnp.arange(SEQ, dtype=jnp.int32)[None, :]
    return {
        "x": x,
        "positions": positions,
        "mix_norm": gain((DEPTH, D_MODEL)),
        "ffn_norm": gain((DEPTH, D_MODEL)),
        "ffn_gate": dense((DEPTH, D_MODEL, D_FF), D_MODEL),
        "ffn_up": dense((DEPTH, D_MODEL, D_FF), D_MODEL),
        "ffn_down": dense((DEPTH, D_FF, D_MODEL), D_FF),
        "w_in_even": dense((N_EVEN, D_MODEL, IN_EVEN), D_MODEL),
        "w_out_even": dense((N_EVEN, D_MODEL, D_MODEL), D_MODEL),
        "hgrn_lb_logits": jax.random.normal(next(ks), (N_EVEN + 1, HGRN_K), F32),
        "hgrn_gnorm": gain((N_EVEN, HGRN_V)),
        "sgu_ln_g": gain((N_EVEN, MIX_HALF)),
        "sgu_ln_b": bias((N_EVEN, MIX_HALF)),
        "sgu_w": 0.5 * dense((N_EVEN, SGU_GROUPS, SGU_CHUNK, SGU_CHUNK), SGU_CHUNK),
        "sgu_b": gain((N_EVEN, SGU_GROUPS, SGU_CHUNK)),
        "w_in_odd": dense((N_ODD, D_MODEL, IN_ODD), D_MODEL),
        "w_out_odd": dense((N_ODD, D_MODEL, D_MODEL), D_MODEL),
        "conv_w": dense((N_ODD, CONV_WIDTH, CONV_CH), CONV_WIDTH),
        "conv_b": bias((N_ODD, CONV_CH)),
        "conv_ln_g": gain((N_ODD, CONV_CH)),
        "conv_ln_b": bias((N_ODD, CONV_CH)),
        "mla_q_norm": gain((N_ODD, MLA_Q_RANK)),
        "mla_w_uq": dense((N_ODD, MLA_Q_RANK, MLA_HEADS * (MLA_NOPE + MLA_ROPE)), MLA_Q_RANK),
        "mla_kv_norm": gain((N_ODD, MLA_KV_RANK)),
        "mla_w_ukv": dense((N_ODD, MLA_KV_RANK, MLA_HEADS * (MLA_NOPE + MLA_V)), MLA_KV_RANK),
        "final_norm": gain((D_MODEL,)),
    }


def reference(x, positions, mix_norm, ffn_norm, ffn_gate, ffn_up, ffn_down,
              w_in_even, w_out_even, hgrn_lb_logits, hgrn_gnorm, sgu_ln_g, sgu_ln_b, sgu_w, sgu_b,
              w_in_odd, w_out_odd, conv_w, conv_b, conv_ln_g, conv_ln_b,
              mla_q_norm, mla_w_uq, mla_kv_norm, mla_w_ukv, final_norm):
    lower_bounds = jnp.cumsum(jax.nn.softmax(hgrn_lb_logits.astype(F32), axis=0), axis=0)
    even_splits = [HGRN_K, 2 * HGRN_K, 2 * HGRN_K + HGRN_V, 2 * HGRN_K + 2 * HGRN_V,
                   2 * HGRN_K + 2 * HGRN_V + MIX_HALF]
    odd_splits = [2 * CONV_CH, 2 * CONV_CH + MLA_Q_RANK, 2 * CONV_CH + MLA_Q_RANK + MLA_KV_RANK]
    for layer in range(DEPTH):
        j = layer // 2
        h = rmsnorm(x, mix_norm[layer])
        if layer % 2 == 0:
            p = h @ w_in_even[j]
            q, f_pre, i, g, u, v = jnp.split(p, even_splits, axis=-1)
            a_out = hgrn2(q, f_pre, i, g, lower_bounds[j], hgrn_gnorm[j])
            b_out = spatial_gating(jax.nn.gelu(u), jax.nn.gelu(v), sgu_ln_g[j], sgu_ln_b[j],
                                   sgu_w[j], sgu_b[j])
            x = x + jnp.concatenate([a_out, b_out], axis=-1) @ w_out_even[j]
        else:
            p = h @ w_in_odd[j]
            h_glu, c_q, c_kv, k_rope_pre = jnp.split(p, odd_splits, axis=-1)
            c_out = conformer_conv(h_glu, conv_w[j], conv_b[j], conv_ln_g[j], conv_ln_b[j])
            d_out = mla(c_q, c_kv, k_rope_pre, positions, mla_q_norm[j], mla_w_uq[j],
                        mla_kv_norm[j], mla_w_ukv[j])
            x = x + jnp.concatenate([c_out, d_out], axis=-1) @ w_out_odd[j]
        h = rmsnorm(x, ffn_norm[layer])
        x = x + swiglu(h, ffn_gate[layer], ffn_up[layer], ffn_down[layer])
    return rmsnorm(x, final_norm)
```

```python
import numpy as np
from contextlib import ExitStack
import concourse.bass as bass
import concourse.mybir as mybir
from concourse.bass_utils import run_bass_kernel_spmd

F32 = mybir.dt.float32
BF16 = mybir.dt.bfloat16
I32 = mybir.dt.int32
AF = mybir.ActivationFunctionType
ALU = mybir.AluOpType
AX = mybir.AxisListType

NCORES = 8
SEQ = 2048
D = 1024
DFF = 2816
T = 512
NSUB = T // 128
NTILE = SEQ // T
EPS = 1e-6
ATT_SCALE = 192.0 ** -0.5
STRICT = True


class Buf:
    __slots__ = ("name", "w", "r", "children")

    def __init__(self, name, children=()):
        self.name = name
        self.w = {}
        self.r = {}
        self.children = list(children)


class Op:
    __slots__ = ("eng", "fn", "deps", "signal", "seq", "chan", "val")

    def __init__(self, eng, fn):
        self.eng = eng
        self.fn = fn
        self.deps = []
        self.signal = False
        self.seq = 0
        self.chan = None
        self.val = 0


class Sched:
    ENGS = ("pe", "act", "dve", "pool", "sp")

    def __init__(self, nc, es):
        self.nc = nc
        self.es = es
        self.streams = {e: [] for e in self.ENGS}
        self.sems = {}
        self.chan_cnt = {}
        for e in ("pe", "act", "dve", "pool"):
            self.sems[e] = es.enter_context(nc.semaphore("s_" + e))
        self.out_chans = set()

    def _chan(self, chan):
        if chan not in self.sems:
            self.sems[chan] = self.es.enter_context(self.nc.semaphore("c_" + str(chan)))
            self.chan_cnt[chan] = 0
        return self.sems[chan]

    def _dep(self, op, key, ref):
        if not isinstance(ref, int):
            if ref.eng == op.eng and op.eng == "pe":
                return
            ref.signal = True
        op.deps.append((key, ref))

    @staticmethod
    def _expand(bufs):
        out = []
        for b in bufs:
            out.append(b)
            out.extend(b.children)
        return out

    def _track(self, op, key, ref, reads, writes, compute):
        reads = self._expand(reads)
        writes = self._expand(writes)
        for b in reads:
            for k, r in b.w.items():
                self._dep(op, k, r)
        for b in writes:
            for k, r in b.w.items():
                if compute and k == op.eng and not STRICT:
                    continue
                self._dep(op, k, r)
            for k, r in b.r.items():
                if compute and k == op.eng and not STRICT:
                    continue
                self._dep(op, k, r)
        for b in reads:
            b.r[key] = ref
        for b in writes:
            b.w = {key: ref}
            b.r = {}

    def op(self, eng, fn, reads=(), writes=()):
        o = Op(eng, fn)
        self._track(o, eng, o, reads, writes, True)
        self.streams[eng].append(o)
        return o

    def dma(self, q, out, in_, reads=(), writes=(), chan=None, is_output=False):
        self._chan(chan)
        o = Op(q, lambda e: e.dma_start(out=out, in_=in_))
        o.chan = chan
        self.chan_cnt[chan] += 16
        o.val = self.chan_cnt[chan]
        self._track(o, chan, o.val, reads, writes, False)
        self.streams[q].append(o)
        if is_output:
            self.out_chans.add(chan)
        return o

    def finalize(self):
        nc = self.nc
        fin = Op("sp", None)
        for ch in sorted(self.out_chans, key=str):
            fin.deps.append((ch, self.chan_cnt[ch]))
        self.streams["sp"].append(fin)
        for e in ("pe", "act", "dve", "pool"):
            n = 0
            for o in self.streams[e]:
                if o.chan is None and o.signal:
                    n += 1
                    o.seq = n
        sems = self.sems
        streams = self.streams
        stats = {}

        def mk(ename):
            def body(eng):
                seen = {}
                nw = 0
                for o in streams[ename]:
                    for key, ref in o.deps:
                        v = ref if isinstance(ref, int) else ref.seq
                        assert v > 0, (ename, key)
                        if seen.get(key, 0) >= v:
                            continue
                        seen[key] = v
                        eng.wait_ge(sems[key], v)
                        nw += 1
                    if o.fn is None:
                        continue
                    ins = o.fn(eng)
                    if o.chan is not None:
                        ins.then_inc(sems[o.chan], 16)
                    elif o.signal:
                        ins.then_inc(sems[ename], 1)
                stats[ename] = (len(streams[ename]), nw)
            return body

        with nc.Block() as block:
            block.tensor(mk("pe"))
            block.scalar(mk("act"))
            block.vector(mk("dve"))
            block.gpsimd(mk("pool"))
            block.sync(mk("sp"))
        return stats


def _col_layout():
    lay = {}
    off = 0

    def add(name, n):
        nonlocal off
        lay[name] = (off, n)
        off += n

    for nm in ("mixn0", "mixn1", "ffnn0", "ffnn1"):
        add(nm, 8)
    add("l0", 4)
    add("l1", 4)
    add("gnorm", 4)
    add("convb", 4)
    add("clng", 4)
    add("clnb", 4)
    add("qnorm", 3)
    add("kvnorm", 2)
    add("sgulg", 4)
    add("sgulb", 4)
    add("convw", 4 * 31)
    add("invs", 1)
    add("sgn2pi", 1)
    add("eps", 1)
    add("zero", 1)
    add("quart", 1)
    return lay, off


COLS, NCOL = _col_layout()


def _fm(v, nchunk):
    return np.ascontiguousarray(np.asarray(v, np.float32).reshape(nchunk, 128).T)


def build_program(layers=(0, 1), ntiles_total=2 * NTILE, taps=None, final_norm=True):
    taps = taps or {}
    nc = bass.Bass("TRN2", target_bir_lowering=False)

    def din(name, shape, dt=F32):
        return nc.dram_tensor(name, list(shape), dt, kind="ExternalInput").ap()

    x_d = din("x", [2 * SEQ, D])
    pos_d = din("pos", [2, SEQ], I32)
    cols_d = din("cols", [128, NCOL])
    consts_d = din("consts", [128, 3, 128])
    finbc_d = din("fin_norm", [1, D])
    sgubias_d = din("sgu_bias", [1, 512])
    sguw_d = din("sgu_wT", [128, 4, 128])
    w_in0_d = din("w_in0", [D, 3072])
    w_out0_d = din("w_out0", [D, D])
    w_in1_d = din("w_in1", [D, 1792])
    w_out1_d = din("w_out1", [D, D])
    wuq_d = din("wuq", [384, 1024])
    wukv_d = din("wukv", [256, 1024])
    wg_d = [din("wg0", [D, DFF]), din("wg1", [D, DFF])]
    wu_d = [din("wu0", [D, DFF]), din("wu1", [D, DFF])]
    wd_d = [din("wd0", [DFF, D]), din("wd1", [DFF, D])]
    out_d = nc.dram_tensor("out", [2 * SEQ, D], F32, kind="ExternalOutput").ap()
    tap_d = {}
    for name, (shape, dt) in taps.items():
        tap_d[name] = nc.dram_tensor("tap_" + name, list(shape), dt, kind="ExternalOutput").ap()

    es = ExitStack()
    with es:
        S = Sched(nc, es)

        def sb(name, shape, dt):
            return es.enter_context(nc.sbuf_tensor("sb_" + name, list(shape), dt))

        def MM(out, lhsT, rhs, st, sp, R, W):
            S.op("pe", lambda e: e.matmul(out, lhsT=lhsT, rhs=rhs, start=st, stop=sp, skip_group_check=True), R, W)

        def TR(out, in_, ident, R, W):
            S.op("pe", lambda e: e.transpose(out=out, in_=in_, identity=ident), R, W)

        def ACT(out, in_, func, R, W, **kw):
            S.op("act", lambda e: e.activation(out=out, in_=in_, func=func, **kw), R, W)

        def TT(out, in0, in1, op, R, W, eng="dve"):
            S.op(eng, lambda e: e.tensor_tensor(out=out, in0=in0, in1=in1, op=op), R, W)

        def TS(out, in0, s1, op0, R, W, s2=None, op1=None, eng="dve"):
            if op1 is None:
                S.op(eng, lambda e: e.tensor_scalar(out=out, in0=in0, scalar1=s1, scalar2=None, op0=op0), R, W)
            else:
                S.op(eng, lambda e: e.tensor_scalar(out=out, in0=in0, scalar1=s1, scalar2=s2, op0=op0, op1=op1), R, W)

        def STT(out, in0, scalar, in1, op0, op1, R, W, eng="dve"):
            S.op(eng, lambda e: e.scalar_tensor_tensor(out=out, in0=in0, scalar=scalar, in1=in1, op0=op0, op1=op1), R, W)

        def CP(out, in_, R, W, eng="dve"):
            S.op(eng, lambda e: e.tensor_copy(out=out, in_=in_), R, W)

        def MEMSET(ap, val, W, eng="dve"):
            S.op(eng, lambda e: e.memset(ap, val), (), W)

        def SCAN(out, d0, d1, R, W):
            S.op("dve", lambda e: e.tensor_tensor_scan(out=out, data0=d0, data1=d1, initial=0.0, op0=ALU.mult, op1=ALU.add), R, W)

        def RSUM(out, in_, R, W):
            S.op("dve", lambda e: e.reduce_sum(out=out, in_=in_, axis=AX.X), R, W)

        def RECIP(out, in_, R, W):
            S.op("dve", lambda e: e.reciprocal(out=out, in_=in_), R, W)

        def tap(name, ap, buf):
            if name in tap_d:
                S.dma("sp", tap_d[name], ap, reads=[buf], chan="tap", is_output=True)

        psum = [es.enter_context(nc.psum_tensor("ps%d" % i, [128, 512], F32)) for i in range(8)]
        Bps = [Buf("ps%d" % i) for i in range(8)]
        ring_state = {"i": 0}

        def pp():
            i = ring_state["i"]
            ring_state["i"] = (i + 1) % 4
            return psum[i], Bps[i]

        def named(i):
            return psum[4 + i], Bps[4 + i]

        cols = sb("cols", [128, NCOL], F32); Bcols = Buf("cols")

        def col(name, i=0, n=1, p=128):
            o, _ = COLS[name]
            return cols[0:p, o + i:o + i + n]

        consts = sb("consts", [128, 3, 128], BF16); Bconst = Buf("consts")
        ident = consts[:, 0, :]
        maskHG = consts[:, 1, :]
        maskAT = consts[:, 2, :]
        onesb = sb("onesb", [128, 128], BF16)
        rmask = sb("rmask", [128, T], F32)
        finbc = sb("finbc", [128, D], F32)
        sgubias = sb("sgubias", [128, 512], F32)
        sguw = sb("sguw", [128, 4, 128], BF16)
        wuq = sb("wuq", [128, 3, 1024], BF16)
        wukv = sb("wukv", [128, 2, 1024], BF16)
        lbc = sb("lbc", [128, 8], F32)
        Bres = Buf("resident")

        xr = sb("xr", [128, NSUB, D], F32); Bxr = [Buf("xr%d" % i) for i in range(NSUB)]
        hT = sb("hT", [128, 8, T], BF16); BhT = [Buf("hT0"), Buf("hT1")]
        Fb = [sb("F%d" % i, [128, 4, T], F32) for i in range(3)]
        BFh = [[Buf("F%d_%d" % (i, h)) for h in range(4)] for i in range(3)]
        BF = [Buf("F%d" % i, BFh[i]) for i in range(3)]
        Hb = [sb("H%d" % i, [128, 4, T], BF16) for i in range(6)]
        BH = [Buf("H%d" % i) for i in range(6)]
        mixT = sb("mixT", [128, 8, T], BF16); Bmix = [Buf("mix%d" % i) for i in range(8)]
        Kn = sb("Kn", [128, 4, SEQ], BF16); BKn = Buf("Kn")
        kr = sb("kr", [64, SEQ], BF16); Bkr = Buf("kr")
        Vc = sb("Vc", [128, SEQ // 128, 512], BF16); BV = Buf("Vc")
        HG = sb("HG", [128, 4, 32 + T], BF16); BHG = Buf("HG")
        Mb = [sb("M%d" % i, [128, T], F32) for i in range(4)]
        BM = [Buf("M%d" % i) for i in range(4)]
        CS = sb("CS", [64, T], F32); SN = sb("SN", [64, T], F32); Brope = Buf("rope")
        Sst = sb("Sst", [128, 4, 128], F32); BS = [Buf("S%d" % i) for i in range(4)]
        hs = [sb("hs%d" % i, [128, D], BF16) for i in range(3)]; Bhs = [Buf("hs%d" % i) for i in range(3)]
        nsm = sb("nsm", [128, 8], F32); Bn = [Buf("n%d" % i) for i in range(NSUB)]
        nsf = sb("nsf", [128, 8], F32); Bnf = [Buf("nf%d" % i) for i in range(NSUB)]
        iki = sb("iki", [64, T], I32); Biki = Buf("iki")
        Bsmh = [Buf("small_h%d" % h) for h in range(4)]
        small = sb("small", [128, 256], F32); Bsmall = Buf("small", Bsmh)
        NWS = 4
        wring = [sb("wr%d" % i, [128, 8, 512], BF16) for i in range(NWS)]
        Bw = [Buf("wr%d" % i) for i in range(NWS)]

        def mkring(name, n, shape, dt):
            ts = [sb("%s%d" % (name, i), shape, dt) for i in range(n)]
            bs = [Buf("%s%d" % (name, i)) for i in range(n)]
            st = {"i": 0}

            def nxt():
                i = st["i"]
                st["i"] = (i + 1) % n
                return ts[i], bs[i]
            return nxt

        r_bf128 = mkring("rb", 8, [128, 128], BF16)
        r_bf512 = mkring("re", 4, [128, T], BF16)
        r_f512 = mkring("rz", 2, [128, T], F32)

        xstage = [mixT[:, 0:4, :].rearrange("p a b -> p (a b)").bitcast(F32),
                  mixT[:, 4:8, :].rearrange("p a b -> p (a b)").bitcast(F32),
                  Hb[4][:].rearrange("p a b -> p (a b)").bitcast(F32),
                  Hb[5][:].rearrange("p a b -> p (a b)").bitcast(F32)]
        xstage_b = [Bmix[0:4], Bmix[4:8], [BH[4]], [BH[5]]]

        wq = {"blocks": [], "issued": 0, "taken": 0, "released": 0}

        def wplan(dram, r0, nk, c0, ncols):
            wq["blocks"].append((dram, r0, nk, c0, ncols))

        def _wissue(i):
            dram, r0, nk, c0, ncols = wq["blocks"][i]
            slot = i % NWS
            src = dram[r0:r0 + nk * 128, c0:c0 + ncols].rearrange("(k p) n -> p k n", p=128)
            S.dma("pool", wring[slot][:, 0:nk, 0:ncols], src, writes=[Bw[slot]], chan="w%d" % slot)

        def wget(n=1):
            wq["released"] = wq["taken"]
            while wq["issued"] < min(len(wq["blocks"]), wq["released"] + NWS):
                _wissue(wq["issued"])
                wq["issued"] += 1
            res = []
            for _ in range(n):
                i = wq["taken"]
                assert i < wq["issued"]
                wq["taken"] += 1
                res.append((wring[i % NWS], Bw[i % NWS]))
            return res[0] if n == 1 else res

        def plan_tile():
            if 0 in layers:
                for c0 in (512, 1024, 0, 1536, 2560, 2048):
                    wplan(w_in0_d, 0, 8, c0, 512)
                for cb in range(2):
                    wplan(w_out0_d, 0, 8, cb * 512, 512)
                plan_ffn(0)
            if 1 in layers:
                wplan(w_in1_d, 0, 8, 0, 512)
                wplan(w_in1_d, 0, 8, 512, 512)
                wplan(w_in1_d, 0, 8, 1024, 384)
                wplan(w_in1_d, 0, 8, 1408, 384)
                for cb in range(2):
                    wplan(w_out1_d, 0, 8, cb * 512, 512)
                plan_ffn(1)

        def plan_ffn(l):
            for cb in range(6):
                ncw = 512 if cb < 5 else 256
                wplan(wg_d[l], 0, 8, cb * 512, ncw)
                wplan(wu_d[l], 0, 8, cb * 512, ncw)
            for cb in range(2):
                for kg, nk in enumerate((8, 8, 6)):
                    wplan(wd_d[l], kg * 1024, nk, cb * 512, 512)

        for _ in range(ntiles_total):
            plan_tile()

        S.dma("sp", cols[:], cols_d, writes=[Bcols], chan="c0")
        S.dma("pool", consts[:], consts_d, writes=[Bconst], chan="c1")
        S.dma("sp", finbc[:], finbc_d.partition_broadcast(128), writes=[Bres], chan="c2")
        S.dma("sp", sgubias[:], sgubias_d.partition_broadcast(128), writes=[Bres], chan="c2")
        S.dma("pool", sguw[:], sguw_d, writes=[Bres], chan="c3")
        S.dma("pool", wuq[:], wuq_d.rearrange("(k p) n -> p k n", p=128), writes=[Bres], chan="c3")
        S.dma("pool", wukv[:], wukv_d.rearrange("(k p) n -> p k n", p=128), writes=[Bres], chan="c3")
        MEMSET(onesb[:], 1.0, [Bres])
        MEMSET(rmask[:], 1.0, [Bres])
        MEMSET(rmask[:].rearrange("p (c t) -> p c t", t=64)[:, :, 0], 0.0, [Bres])
        TT(sguw[:], sguw[:], consts[:, 2:3, :].to_broadcast([128, 4, 128]), ALU.mult, [Bres, Bconst], [Bres])
        pw, pwb = pp()
        MM(pw[:, :], onesb[:], sguw[:].rearrange("p g t -> p (g t)"), True, True, [Bres], [pwb])
        for g in range(4):
            STT(sgubias[:, g * 128:(g + 1) * 128], pw[:, g * 128:(g + 1) * 128], col("sgulb", g), sgubias[:, g * 128:(g + 1) * 128],
                ALU.mult, ALU.add, [pwb, Bcols, Bres], [Bres])
        TT(lbc[:, 0:4], col("l0", 0, 4), col("l1", 0, 4), ALU.subtract, [Bcols], [Bres])
        ACT(lbc[:, 4:8], lbc[:, 0:4], AF.Sigmoid, [Bres], [Bres], scale=-1.0)
        ACT(lbc[:, 0:4], lbc[:, 0:4], AF.Sigmoid, [Bres], [Bres])

        def rms_rstd(ssq_ap, out_ap, n, inv_n, R, W):
            ACT(out_ap, ssq_ap, AF.Ln, R, W, scale=inv_n, bias=col("eps", p=n))
            ACT(out_ap, out_ap, AF.Exp, W, W, scale=-0.5)

        def xsrc(sub, staged):
            return (xstage[sub], xstage_b[sub]) if staged else (xr[:, sub, :], [Bxr[sub]])

        def norm_pre1(sub, staged=False):
            h_t, h_b = hs[sub % 3], Bhs[sub % 3]
            xa, xb = xsrc(sub, staged)
            MEMSET(nsm[:, sub:sub + 1], 0.0, [Bn[sub]])
            ACT(h_t[:], xa, AF.Square, xb, [h_b, Bn[sub]], accum_out=nsm[:, sub:sub + 1])
            ACT(nsm[:, 4 + sub:5 + sub], nsm[:, sub:sub + 1], AF.Ln, [Bn[sub], Bcols], [Bn[sub]], scale=1.0 / D, bias=col("eps"))
            ACT(nsm[:, 4 + sub:5 + sub], nsm[:, 4 + sub:5 + sub], AF.Exp, [Bn[sub]], [Bn[sub]], scale=-0.5)

        def norm_pre2(sub, staged=False):
            h_t, h_b = hs[sub % 3], Bhs[sub % 3]
            xa, xb = xsrc(sub, staged)
            if sub % 2 == 0:
                ACT(h_t[:], xa, AF.Copy, xb + [Bn[sub]], [h_b], scale=nsm[:, 4 + sub:5 + sub])
            else:
                TS(h_t[:], xa, nsm[:, 4 + sub:5 + sub], ALU.mult, xb + [Bn[sub]], [h_b])

        def norm_post(sub, normname):
            h_t, h_b = hs[sub % 3], Bhs[sub % 3]
            pt, pb = pp()
            ptb = pt[:].bitcast(BF16)
            for c in range(8):
                TR(ptb[:, c * 128:(c + 1) * 128], h_t[:, c * 128:(c + 1) * 128], ident, [h_b, Bconst], [pb])
            o, _ = COLS[normname]
            TT(hT[:, :, sub * 128:(sub + 1) * 128], ptb.rearrange("p (c t) -> p c t", c=8),
               cols[:, o:o + 8].unsqueeze(2).to_broadcast([128, 8, 128]), ALU.mult, [pb, Bcols], [BhT[sub // 2]])

        def norm_hook(normname, staged=False):
            def after(sub):
                norm_pre1(sub, staged)
                if sub >= 1:
                    norm_pre2(sub - 1, staged)
                if sub >= 2:
                    norm_post(sub - 2, normname)
                if sub == NSUB - 1:
                    norm_pre2(sub, staged)
                    norm_post(sub - 1, normname)
                    norm_post(sub, normname)
            return after

        def norm_to_hT(normname, staged=False):
            hk = norm_hook(normname, staged)
            for sub in range(NSUB):
                hk(sub)

        def proj_fm(wt, wb, c0, m, nk=8, mcols=128, rhs=None, rhsb=None, nrows=128, halves=False):
            pt, pb = pp()
            if rhs is None:
                if halves:
                    for hf in range(2):
                        csl = slice(hf * 256, (hf + 1) * 256)
                        for k in range(nk):
                            MM(pt[0:mcols, csl], wt[0:nrows, k, c0 + m * 128:c0 + m * 128 + mcols], hT[0:nrows, k, csl], k == 0, k == nk - 1, [wb, BhT[hf]], [pb])
                    return pt, pb
                rhs, rb_ = hT, BhT
            else:
                rb_ = [rhsb]
            for k in range(nk):
                MM(pt[0:mcols, :], wt[0:nrows, k, c0 + m * 128:c0 + m * 128 + mcols], rhs[0:nrows, k, :], k == 0, k == nk - 1, [wb] + rb_, [pb])
            return pt, pb

        def proj_tm(wt, wb, c0, ncols, sub, nk=8, lhs=None, lhsb=None):
            if lhs is None:
                lhs, lhsb = hT, BhT[sub // 2]
            pt, pb = pp()
            for k in range(nk):
                MM(pt[:, 0:ncols], lhs[:, k, sub * 128:(sub + 1) * 128], wt[:, k, c0:c0 + ncols], k == 0, k == nk - 1, [wb, lhsb], [pb])
            return pt, pb

        def out_proj(after=None, korder=(0, 1, 2, 3, 4, 5, 6, 7)):
            wts = wget(2)
            for sub in range(NSUB):
                for cb in range(2):
                    wt, wb = wts[cb]
                    pt, pb = pp()
                    for ki, k in enumerate(korder):
                        MM(pt[:, :], mixT[:, k, sub * 128:(sub + 1) * 128], wt[:, k, 0:512], ki == 0, ki == 7, [wb, Bmix[k]], [pb])
                    TT(xr[:, sub, cb * 512:(cb + 1) * 512], xr[:, sub, cb * 512:(cb + 1) * 512], pt[:, :], ALU.add, [pb, Bxr[sub]], [Bxr[sub]])
                if after is not None:
                    after(sub)

        def ffn(l, after=None, prefetch=None):
            for cb in range(6):
                if cb == 1 and prefetch is not None:
                    prefetch()
                nm = 4 if cb < 5 else 2
                (wgt, wgb), (wut, wub) = wget(2)
                for m in range(nm):
                    pg, pgb = proj_fm(wgt, wgb, 0, m, halves=(cb == 0 and m == 0))
                    pu, pub = proj_fm(wut, wub, 0, m)
                    sl, slb = r_bf512()
                    ACT(sl[:], pg[:, :], AF.Silu, [pgb], [slb])
                    ch = cb * 4 + m
                    TT(actv(ch), sl[:], pu[:, :], ALU.mult, [slb, pub], actb(ch))
            accs = [named(i) for i in range(4)]
            for kg, nk in enumerate((8, 8, 6)):
                wt, wb = wget()
                for sub in range(NSUB):
                    at, ab = accs[sub]
                    for k in range(nk):
                        ch = kg * 8 + k
                        MM(at[:, :], actv(ch)[:, sub * 128:(sub + 1) * 128], wt[:, k, 0:512], ch == 0, ch == 21, [wb] + actb(ch), [ab])
                    if kg == 2:
                        TT(xr[:, sub, 0:512], xr[:, sub, 0:512], at[:, :], ALU.add, [ab, Bxr[sub]], [Bxr[sub]])
            wt, wb = wget()
            for sub in range(NSUB):
                at, ab = accs[sub]
                for k in range(8):
                    MM(at[:, :], actv(k)[:, sub * 128:(sub + 1) * 128], wt[:, k, 0:512], k == 0, False, [wb] + actb(k), [ab])
            wts = wget(2)
            for sub in range(NSUB):
                at, ab = accs[sub]
                for kg, nk in ((1, 8), (2, 6)):
                    wt, wb = wts[kg - 1]
                    for k in range(nk):
                        ch = kg * 8 + k
                        MM(at[:, :], actv(ch)[:, sub * 128:(sub + 1) * 128], wt[:, k, 0:512], False, ch == 21, [wb] + actb(ch), [ab])
                TT(xr[:, sub, 512:1024], xr[:, sub, 512:1024], at[:, :], ALU.add, [ab, Bxr[sub]], [Bxr[sub]])
                if after is not None:
                    after(sub)

        Fbf = [Fb[i][:].rearrange("p a b -> p (a b)").bitcast(BF16) for i in range(3)]

        def actv(ch):
            return Fbf[ch // 8][:, (ch % 8) * T:(ch % 8 + 1) * T]

        def actb(ch):
            return [BF[ch // 8]]

        def layer0(first_in_seq, ffn_after, ffn_prefetch=None):
            F1, F2, F3 = Fb
            B1, B2, B3 = BF
            if first_in_seq:
                MEMSET(Sst[:], 0.0, BS)
            norm_to_hT("mixn0", staged=True)
            tap("hT0", hT[:], BhT[1])
            wt, wb = wget()
            for h in range(4):
                pt, pb = proj_fm(wt, wb, 0, h, halves=(h == 0))
                ACT(F1[:, h, :], pt[:, :], AF.Sigmoid, [pb], [B1])
            (wti, wbi), (wtq, wbq) = wget(2)
            B1h, B2h, B3h = BFh
            for h in range(4):
                TS(F2[:, h, :], F1[:, h, :], -1.0, ALU.mult, [B1h[h]], [B2h[h]], s2=1.0, op1=ALU.add)
                ACT(F1[:, h, :], F1[:, h, :], AF.Ln, [B1h[h], Bres], [B1h[h]], scale=lbc[:, 4 + h:5 + h], bias=lbc[:, h:h + 1])
                SCAN(F3[:, h, :], rmask[:], F1[:, h, :], [B1h[h], Bres], [B3h[h]])
                pt, pb = proj_tm(wti, wbi, 0, 512, h)
                ACT(Hb[4][:, h, :], pt[:, :], AF.Copy, [pb], [BH[4]])
                bh = F3[:, h, :].rearrange("p (c t) -> p c t", t=64)
                hs8 = slice(h * 8, (h + 1) * 8)
                ACT(small[:, 16:48][:, hs8], bh[:, :, 63], AF.Exp, [B3h[h]], [Bsmh[h]])
                ACT(small[:, 48:80][:, hs8], bh[:, :, 31], AF.Exp, [B3h[h]], [Bsmh[h]])
                CP(small[:, 80:112][:, hs8], bh[:, :, 31], [B3h[h]], [Bsmh[h]])
                TT(bh, bh, small[:, 80:112][:, hs8].unsqueeze(2).to_broadcast([128, 8, 64]), ALU.subtract, [B3h[h], Bsmh[h]], [B3h[h]])
                ACT(F1[:, h, :], F3[:, h, :], AF.Exp, [B3h[h]], [B1h[h]])
                ACT(F3[:, h, :], F3[:, h, :], AF.Exp, [B3h[h]], [B3h[h]], scale=-1.0)
                pt, pb = proj_fm(wtq, wbq, 0, h)
                TT(Hb[0][:, h, :], pt[:, :], F1[:, h, :], ALU.mult, [pb, B1h[h]], [BH[0]])
                STT(Hb[1][:, h, :], F2[:, h, :], lbc[:, 4 + h:5 + h], F3[:, h, :], ALU.mult, ALU.mult, [B2h[h], B3h[h], Bres], [BH[1]])
            wtg, wbg = wget()
            for h in range(4):
                pt, pb = proj_fm(wtg, wbg, 0, h)
                ACT(Hb[2][:, h, :], pt[:, :], AF.Silu, [pb], [BH[2]])
            tap("qp", Hb[0][:], BH[0])
            tap("kp", Hb[1][:], BH[1])
            for sub in range(NSUB):
                pt, pb = pp()
                ptb = pt[:].bitcast(BF16)
                for h in range(4):
                    TR(ptb[:, h * 128:(h + 1) * 128], Hb[1][:, h, sub * 128:(sub + 1) * 128], ident, [BH[1], Bconst], [pb])
                ACT(Hb[5][:, sub, :], ptb[:, 0:512], AF.Copy, [pb], [BH[5]])
            kTM, iTM, qT, kT = Hb[5], Hb[4], Hb[0], Hb[1]
            pO = [named(h) for h in range(4)]
            TUv = [F2[:].rearrange("p a (b v) -> p (a b) v", v=128), F3[:].rearrange("p a (b v) -> p (a b) v", v=128)]
            Btu = [B2, B3]
            SBv = [Hb[1][:].rearrange("p a (b v) -> p (a b) v", v=128), Hb[5][:].rearrange("p a (b v) -> p (a b) v", v=128)]
            Bsb = [BH[1], BH[5]]
            E1v = small[:, 16:48].rearrange("p (h c) -> p h c", c=8)
            E3v = small[:, 48:80].rearrange("p (h c) -> p h c", c=8)

            def TU(c):
                return TUv[c // 4][:, (c % 4) * 4:(c % 4) * 4 + 4, :]

            def SBF(c):
                return SBv[c // 4][:, (c % 4) * 4:(c % 4) * 4 + 4, :]

            for sub in range(NSUB):
                tsl = slice(sub * 128, (sub + 1) * 128)
                for h in range(4):
                    hsl = slice(h * 128, (h + 1) * 128)
                    po, pob = pO[h]
                    psS, psSb = pp()
                    MM(psS[:, 0:128], kT[:, h, tsl], qT[:, h, tsl], True, True, [BH[1], BH[0]], [psSb])
                    TT(mixT[:, 4 + sub, hsl], psS[:, 0:128], maskHG, ALU.mult, [psSb, Bconst], [Bmix[4 + sub]])
                    for cc in range(2):
                        c = sub * 2 + cc
                        ps_ = slice(cc * 64, cc * 64 + 64)
                        pu, pub = pp()
                        MM(pu[:, 0:128], kTM[ps_, sub, hsl], iTM[ps_, sub, hsl], True, True, [BH[5], BH[4]], [pub])
                        ACT(TU(c)[:, h, :], pu[:, 0:128], AF.Copy, [pub, B1], [Btu[c // 4]], scale=F1[:, h, c * 64 + 63:c * 64 + 64])
            for c in range(T // 64):
                TT(SBF(c), Sst[:], E3v[:, :, c].unsqueeze(2).to_broadcast([128, 4, 128]), ALU.mult, BS + [Bsmall], [Bsb[c // 4]])
                for h in range(4):
                    STT(Sst[:, h, :], Sst[:, h, :], E1v[:, h, c:c + 1], TU(c)[:, h, :], ALU.mult, ALU.add, [BS[h], Bsmall, Btu[c // 4]], [BS[h]])
            wt, wb = wget()
            for sub in range(NSUB):
                pt, pb = proj_tm(wt, wb, 0, 512, sub)
                ACT(F1[:, sub, :], pt[:, :], AF.Gelu_apprx_tanh, [pb], [B1])
            wt, wb = wget()
            for g in range(4):
                pt, pb = proj_fm(wt, wb, 0, g)
                ACT(Hb[3][:, g, :], pt[:, :], AF.Gelu_apprx_tanh, [pb], [BH[3]])
            for sub in range(NSUB):
                tsl = slice(sub * 128, (sub + 1) * 128)
                for h in range(4):
                    hsl = slice(h * 128, (h + 1) * 128)
                    po, pob = pO[h]
                    MM(po[:, tsl], iTM[:, sub, hsl], mixT[:, 4 + sub, hsl], True, False, [BH[4], Bmix[4 + sub]], [pob])
                    for cc in range(2):
                        c = sub * 2 + cc
                        t0 = sub * 128 + cc * 64
                        MM(po[:, t0:t0 + 64], SBF(c)[:, h, :], qT[:, h, t0:t0 + 64], False, cc == 1, [Bsb[c // 4], BH[0]], [pob])
            osqs = []
            for h in range(4):
                po, pob = pO[h]
                osq, osqb = r_bf512()
                ACT(osq[:], po[:, :], AF.Square, [pob], [osqb])
                osqs.append((osq, osqb))
            pns = []
            for h in range(4):
                osq, osqb = osqs[h]
                pn, pnb = pp()
                MM(pn[:, :], onesb[:], osq[:], True, True, [Bres, osqb], [pnb])
                pns.append((pn, pnb))
            r4 = [(Mb[i], BM[i]) for i in range(4)]
            for h in range(4):
                pn, pnb = pns[h]
                rms_rstd(pn[:, :], r4[h][0][:], 128, 1.0 / 128, [pnb, Bcols], [r4[h][1]])
            v16 = F1[:].rearrange("p s (g c) -> p (s g) c", c=128)
            sums = small[:, 112:128]
            sumsq = small[:, 128:144]
            mean = small[:, 144:160]
            rstd = small[:, 160:176]
            RSUM(sums, v16, [B1], [Bsmall])
            TT(F2[:], F1[:], F1[:], ALU.mult, [B1], [B2])
            RSUM(sumsq, F2[:].rearrange("p s (g c) -> p (s g) c", c=128), [B2], [Bsmall])
            TS(mean, sums, 1.0 / 128, ALU.mult, [Bsmall], [Bsmall])
            TT(v16, v16, mean.unsqueeze(2).to_broadcast([128, 16, 128]), ALU.subtract, [B1, Bsmall], [B1])
            TT(sums, mean, mean, ALU.mult, [Bsmall], [Bsmall])
            STT(sumsq, sumsq, 1.0 / 128, sums, ALU.mult, ALU.subtract, [Bsmall], [Bsmall])
            rms_rstd(sumsq, rstd, 128, 1.0, [Bsmall, Bcols], [Bsmall])
            TT(Hb[4][:].rearrange("p s (g c) -> p (s g) c", c=128), v16, rstd.unsqueeze(2).to_broadcast([128, 16, 128]), ALU.mult, [B1, Bsmall], [BH[4]])
            vn = Hb[4]
            for g in range(4):
                pz, pzb = pp()
                for sub in range(NSUB):
                    MM(pz[:, sub * 128:(sub + 1) * 128], vn[:, sub, g * 128:(g + 1) * 128], sguw[:, g, :], True, True, [BH[4], Bres], [pzb])
                zt, ztb = r_f512()
                STT(zt[:].rearrange("p (s t) -> p s t", s=4), pz[:, :].rearrange("p (s t) -> p s t", s=4), col("sgulg", g),
                    sgubias[:, g * 128:(g + 1) * 128].unsqueeze(1).to_broadcast([128, 4, 128]), ALU.mult, ALU.add, [pzb, Bres, Bcols], [ztb])
                TT(mixT[:, 4 + g, :], zt[:], Hb[3][:, g, :], ALU.mult, [ztb, BH[3]], [Bmix[4 + g]])
            for h in range(4):
                po, pob = pO[h]
                tmpo, tmpob = r_f512()
                TT(tmpo[:], po[:, :], r4[h][0][:], ALU.mult, [pob, r4[h][1]], [tmpob])
                STT(mixT[:, h, :], tmpo[:], col("gnorm", h), Hb[2][:, h, :], ALU.mult, ALU.mult, [tmpob, Bcols, BH[2]], [Bmix[h]])
            tap("aout", mixT[:, 0:4, :], Bmix[3])
            tap("bout", mixT[:, 4:8, :], Bmix[7])
            out_proj(after=norm_hook("ffnn0"), korder=(4, 5, 6, 7, 0, 1, 2, 3))
            tap("x_mix0", xr[:], Bxr[3])
            ffn(0, after=ffn_after, prefetch=ffn_prefetch)

        def layer1(seq, j, first_in_seq, pre_normed, ffn_after, ffn_prefetch=None):
            F1, F2, F3 = Fb
            B1, B2, B3 = BF
            tok0 = j * T
            S.dma("sp", iki[:], pos_d[seq:seq + 1, tok0:tok0 + T].partition_broadcast(64), writes=[Biki], chan="pos")
            if not pre_normed:
                norm_to_hT("mixn1", staged=True)
            if first_in_seq:
                MEMSET(HG[:, :, 0:32], 0.0, [BHG])
            wt, wb = wget()
            for c in range(4):
                pt, pb = proj_fm(wt, wb, 0, c, halves=(c == 0))
                ACT(F2[:, c, :], pt[:, :], AF.Copy, [pb], [B2])
            wt, wb = wget()
            for c in range(4):
                pt, pb = proj_fm(wt, wb, 0, c)
                ACT(F1[:, c, :], pt[:, :], AF.Sigmoid, [pb], [B1])
                TT(HG[:, c, 32:32 + T], F2[:, c, :], F1[:, c, :], ALU.mult, [B1, B2], [BHG])
            Y1 = F3[0:64, 1, :]
            R1 = F3[0:64, 2, :]
            CP(Y1, iki[:], [Biki], [B3])
            TS(Y1, Y1, col("invs", p=64), ALU.mult, [B3, Bcols], [B3])

            def frac_to(dst):
                CP(iki[:], Y1, [B3], [Biki])
                CP(R1, iki[:], [Biki], [B3])
                TT(R1, Y1, R1, ALU.subtract, [B3], [B3])

            frac_to(R1)
            ACT(SN[:], R1, AF.Sin, [B3, Bcols], [Brope], scale=col("sgn2pi", p=64))
            TS(Y1, Y1, 0.25, ALU.add, [B3], [B3])
            frac_to(R1)
            ACT(CS[:], R1, AF.Sin, [B3], [Brope], scale=float(2 * np.pi))
            wt, wb = wget()
            for m in range(3):
                pt, pb = proj_fm(wt, wb, 0, m)
                ACT(F3[:, m, :], pt[:, :], AF.Copy, [pb, Bcols], [B3], scale=col("qnorm", m))
                ACT(Hb[2][:, m, :], pt[:, :], AF.Square, [pb], [BH[2]])
            wt, wb = wget()
            for m in range(2):
                pt, pb = proj_fm(wt, wb, 0, m)
                ACT(F1[:, m, :], pt[:, :], AF.Copy, [pb, Bcols], [B1], scale=col("kvnorm", m))
                ACT(Hb[5][:, m, :], pt[:, :], AF.Square, [pb], [BH[5]])
            pr, prb = proj_fm(wt, wb, 256, 0, mcols=64)
            prp, prpb = proj_fm(wt, wb, 320, 0, mcols=64)
            t1, t1b = r_f512()
            t2, t2b = r_f512()
            TT(t1[0:64, :], pr[0:64, :], CS[:], ALU.mult, [prb, Brope], [t1b])
            TT(t2[0:64, :], prp[0:64, :], SN[:], ALU.mult, [prpb, Brope], [t2b])
            TT(kr[:, tok0:tok0 + T], t1[0:64, :], t2[0:64, :], ALU.add, [t1b, t2b], [Bkr])
            pn, pnb = pp()
            for m in range(3):
                MM(pn[:, :], onesb[:], Hb[2][:, m, :], m == 0, m == 2, [Bres, BH[2]], [pnb])
            pk, pkb = pp()
            for m in range(2):
                MM(pk[:, :], onesb[:], Hb[5][:, m, :], m == 0, m == 1, [Bres, BH[5]], [pkb])
            rms_rstd(pn[:, :], Mb[2][:], 128, 1.0 / 384, [pnb, Bcols], [BM[2]])
            TT(Hb[3][:, 0:3, :], F3[:, 0:3, :], Mb[2][:].unsqueeze(1).to_broadcast([128, 3, T]), ALU.mult, [B3, BM[2]], [BH[3]])
            rms_rstd(pk[:, :], Mb[2][:], 128, 1.0 / 256, [pkb, Bcols], [BM[2]])
            TT(Hb[4][:, 0:2, :], F1[:, 0:2, :], Mb[2][:].unsqueeze(1).to_broadcast([128, 2, T]), ALU.mult, [B1, BM[2]], [BH[4]])
            o_cw, _ = COLS["convw"]
            for c in range(4):
                py, pyb = pp()
                for jt in range(31):
                    dg, dgb = r_bf128()
                    TS(dg[:], ident, cols[:, o_cw + c * 31 + jt:o_cw + c * 31 + jt + 1], ALU.mult, [Bconst, Bcols], [dgb])
                    MM(py[:, :], dg[:], HG[:, c, 2 + jt:2 + jt + T], jt == 0, jt == 30, [dgb, BHG], [pyb])
                ACT(F2[:, c, :], py[:, :], AF.Identity, [pyb, Bcols], [B2], bias=col("convb", c))
                ACT(Hb[0][:, c, :], py[:, :], AF.Identity, [pyb, Bcols], [BH[0]], bias=col("convb", c))
                ACT(Hb[1][:, c, :], py[:, :], AF.Square, [pyb, Bcols], [BH[1]], bias=col("convb", c))
            CP(HG[:, :, 0:32], HG[:, :, T:T + 32], [BHG], [BHG])
            pm, pmb = named(2)
            for c in range(4):
                MM(pm[:, :], onesb[:], Hb[0][:, c, :], c == 0, c == 3, [Bres, BH[0]], [pmb])
            pq, pqb = named(3)
            for c in range(4):
                MM(pq[:, :], onesb[:], Hb[1][:, c, :], c == 0, c == 3, [Bres, BH[1]], [pqb])
            cqn, ckvn = Hb[3], Hb[4]
            qnT, qrT = Hb[0], Hb[1]
            for h in range(4):
                pt, pb = proj_fm(wuq, Bres, 0, h, nk=3, rhs=cqn, rhsb=BH[3])
                ACT(qnT[:, h, :], pt[:, :], AF.Copy, [pb], [BH[0]])
                pr, prb = proj_fm(wuq, Bres, 512 + h * 64, 0, nk=3, mcols=64, rhs=cqn, rhsb=BH[3])
                prp, prpb = proj_fm(wuq, Bres, 768 + h * 64, 0, nk=3, mcols=64, rhs=cqn, rhsb=BH[3])
                t1, t1b = r_f512()
                t2, t2b = r_f512()
                TT(t1[0:64, :], pr[0:64, :], CS[:], ALU.mult, [prb, Brope], [t1b])
                TT(t2[0:64, :], prp[0:64, :], SN[:], ALU.mult, [prpb, Brope], [t2b])
                TT(qrT[0:64, h, :], t1[0:64, :], t2[0:64, :], ALU.add, [t1b, t2b], [BH[1]])
            for h in range(4):
                pt, pb = proj_fm(wukv, Bres, 0, h, nk=2, rhs=ckvn, rhsb=BH[4])
                ACT(Kn[:, h, tok0:tok0 + T], pt[:, :], AF.Copy, [pb], [BKn])
            for sub in range(NSUB):
                pt, pb = proj_tm(wukv, Bres, 512, 512, sub, nk=2, lhs=ckvn, lhsb=BH[4])
                ACT(Vc[:, j * NSUB + sub, :], pt[:, :], AF.Copy, [pb], [BV])
            def conv_ln_chain():
                ACT(Mb[0][:], pm[:, :], AF.Copy, [pmb], [BM[0]], scale=1.0 / 512)
                TT(Mb[1][:], Mb[0][:], Mb[0][:], ALU.mult, [BM[0]], [BM[1]])
                STT(Mb[1][:], pq[:, :], 1.0 / 512, Mb[1][:], ALU.mult, ALU.subtract, [pqb, BM[1]], [BM[1]])
                rms_rstd(Mb[1][:], Mb[1][:], 128, 1.0, [BM[1], Bcols], [BM[1]])
                TT(F2[:], F2[:], Mb[0][:].unsqueeze(1).to_broadcast([128, 4, T]), ALU.subtract, [B2, BM[0]], [B2])
                TT(F2[:], F2[:], Mb[1][:].unsqueeze(1).to_broadcast([128, 4, T]), ALU.mult, [B2, BM[1]], [B2])
                for c in range(4):
                    ACT(mixT[:, c, :], F2[:, c, :], AF.Silu, [B2, Bcols], [Bmix[c]], scale=col("clng", c), bias=col("clnb", c))
                tap("cout", mixT[:, 0:4, :], Bmix[3])
            nkb = j * NSUB + NSUB
            LOOK = 3
            banks = [(named(2 * (h % 2)), named(2 * (h % 2) + 1)) for h in range(4)]

            def s_stage(h, kb):
                s0 = max(0, kb - j * NSUB)
                qsl = slice(s0 * 128, T)
                ksl = slice(kb * 128, (kb + 1) * 128)
                psS, psSb = pp()
                MM(psS[:, qsl], Kn[:, h, ksl], qnT[:, h, qsl], True, False, [BKn, BH[0]], [psSb])
                MM(psS[:, qsl], kr[0:64, ksl], qrT[0:64, h, qsl], False, True, [Bkr, BH[1]], [psSb])
                ex, exb = r_bf512()
                ACT(ex[:, qsl], psS[:, qsl], AF.Exp, [psSb], [exb], scale=ATT_SCALE)
                if kb >= j * NSUB:
                    dsl = slice(s0 * 128, (s0 + 1) * 128)
                    TT(ex[:, dsl], ex[:, dsl], maskAT, ALU.mult, [exb, Bconst], [exb])
                return ex, exb, qsl

            def pv_stage(h, kb, ex, exb, qsl):
                (po, pob), (pd, pdb) = banks[h]
                MM(po[:, qsl], Vc[:, kb, h * 128:(h + 1) * 128], ex[:, qsl], kb == kb_order[0], kb == kb_order[-1], [BV, exb], [pob])
                MM(pd[:, qsl], onesb[:], ex[:, qsl], kb == kb_order[0], kb == kb_order[-1], [Bres, exb], [pdb])
                if kb == kb_order[-1]:
                    ACT(Mb[1][:], pd[:, :], AF.Ln, [pdb], [BM[1]])
                    ACT(Mb[1][:], Mb[1][:], AF.Exp, [BM[1]], [BM[1]], scale=-1.0)
                    TT(mixT[:, 4 + h, :], po[:, :], Mb[1][:], ALU.mult, [pob, BM[1]], [Bmix[4 + h]])

            kb_order = list(range(j * NSUB, nkb)) + list(range(0, j * NSUB))
            pend = []
            for h in range(4):
                if h == 1:
                    conv_ln_chain()
                for kb in kb_order:
                    pend.append(((h, kb), s_stage(h, kb)))
                    if len(pend) > LOOK:
                        it0, st0 = pend.pop(0)
                        pv_stage(*it0, *st0)
            while pend:
                it0, st0 = pend.pop(0)
                pv_stage(*it0, *st0)
            tap("dout", mixT[:, 4:8, :], Bmix[7])
            out_proj(after=norm_hook("ffnn1"))
            tap("x_mix1", xr[:], Bxr[3])
            ffn(1, after=ffn_after, prefetch=ffn_prefetch)

        xv = x_d.rearrange("(n p) d -> p n d", p=128)
        ov = out_d.rearrange("(n p) d -> p n d", p=128)
        F01 = [Fb[0], Fb[1]]
        for ti in range(ntiles_total):
            seq, j = divmod(ti, NTILE)
            first = (j == 0)
            def stage_x(t2):
                for sub in range(NSUB):
                    S.dma("sp", xstage[sub], xv[:, t2 * NSUB + sub, :], writes=xstage_b[sub], chan="x%d" % sub)

            if ti == 0:
                stage_x(0)
            for sub in range(NSUB):
                S.dma("sp", xr[:, sub, :], xstage[sub], reads=xstage_b[sub], writes=[Bxr[sub]], chan="xc%d" % sub)
            nxt = (lambda t2=ti + 1: stage_x(t2)) if ti + 1 < ntiles_total else None
            def final_pre(sub):
                h_t, h_b = hs[sub % 3], Bhs[sub % 3]
                ssq, r0 = nsm[:, sub:sub + 1], nsm[:, 4 + sub:5 + sub]
                MEMSET(ssq, 0.0, [Bn[sub]])
                ACT(h_t[:], xr[:, sub, :], AF.Square, [Bxr[sub]], [h_b, Bn[sub]], accum_out=ssq)
                ACT(r0, ssq, AF.Ln, [Bn[sub], Bcols], [Bn[sub]], scale=1.0 / D, bias=col("eps"))
                ACT(r0, r0, AF.Exp, [Bn[sub]], [Bn[sub]], scale=-0.5)

            def final_post(sub, ti=ti):
                ot = Hb[sub][:].rearrange("p a b -> p (a b)").bitcast(F32)
                STT(ot, xr[:, sub, :], nsm[:, 4 + sub:5 + sub], finbc[:], ALU.mult, ALU.mult, [Bxr[sub], Bn[sub], Bres], [BH[sub]])
                S.dma("sp", ov[:, ti * NSUB + sub, :], ot, reads=[BH[sub]], chan="o%d" % sub, is_output=True)

            def final_sub(sub):
                final_pre(sub)
                if sub >= 1:
                    final_post(sub - 1)
                if sub == NSUB - 1:
                    final_post(sub)

            def raw_sub(sub, ti=ti):
                S.dma("sp", ov[:, ti * NSUB + sub, :], xr[:, sub, :], reads=[Bxr[sub]], chan="o%d" % sub, is_output=True)

            last_after = final_sub if final_norm else raw_sub
            if 0 in layers and 1 in layers:
                layer0(first, norm_hook("mixn1"))
                layer1(seq, j, first, True, last_after, nxt)
            elif 0 in layers:
                layer0(first, last_after, nxt)
            else:
                layer1(seq, j, first, False, last_after, nxt)
        stats = S.finalize()
    return nc, stats


def _host_shared(inp):
    f = lambda a: np.ascontiguousarray(np.asarray(a, dtype=np.float32))
    cols = np.zeros((128, NCOL), np.float32)

    def put(name, arr):
        o, n = COLS[name]
        cols[:, o:o + n] = arr

    put("mixn0", _fm(inp["mix_norm"][0], 8))
    put("mixn1", _fm(inp["mix_norm"][1], 8))
    put("ffnn0", _fm(inp["ffn_norm"][0], 8))
    put("ffnn1", _fm(inp["ffn_norm"][1], 8))
    put("l0", _fm(inp["hgrn_lb_logits"][0], 4))
    put("l1", _fm(inp["hgrn_lb_logits"][1], 4))
    put("gnorm", _fm(inp["hgrn_gnorm"][0], 4))
    put("sgulg", _fm(inp["sgu_ln_g"][0], 4))
    put("sgulb", _fm(inp["sgu_ln_b"][0], 4))
    put("convb", _fm(inp["conv_b"][0], 4))
    put("clng", _fm(inp["conv_ln_g"][0], 4))
    put("clnb", _fm(inp["conv_ln_b"][0], 4))
    put("qnorm", _fm(inp["mla_q_norm"][0], 3))
    put("kvnorm", _fm(inp["mla_kv_norm"][0], 2))
    cw = np.asarray(inp["conv_w"][0], np.float32)
    put("convw", np.ascontiguousarray(cw.T.reshape(4, 128, 31).transpose(1, 0, 2).reshape(128, 124)))
    inv = 1.0 / (10000.0 ** (np.arange(0, 64, 2, dtype=np.float32) / 64.0))
    inv2 = np.concatenate([inv, inv]).astype(np.float32)
    invs = np.zeros(128, np.float32)
    invs[:64] = (inv2.astype(np.float64) / (2 * np.pi)).astype(np.float32)
    put("invs", invs[:, None])
    sg = np.zeros(128, np.float32)
    sg[:32] = -2 * np.pi
    sg[32:64] = 2 * np.pi
    put("sgn2pi", sg[:, None])
    put("eps", np.full((128, 1), EPS, np.float32))
    put("zero", np.zeros((128, 1), np.float32))
    put("quart", np.full((128, 1), 0.25, np.float32))

    p = np.arange(128)
    consts = np.zeros((128, 3, 128), np.float32)
    consts[:, 0, :] = np.eye(128, dtype=np.float32)
    consts[:, 1, :] = ((p[:, None] <= p[None, :]) & ((p[:, None] // 64) == (p[None, :] // 64))).astype(np.float32)
    consts[:, 2, :] = (p[:, None] <= p[None, :]).astype(np.float32)

    w_in1 = np.asarray(inp["w_in_odd"][0], np.float32)
    krope = w_in1[:, 1664:1728]
    w_in1x = np.concatenate([w_in1, krope[:, 32:64], krope[:, 0:32]], axis=1)
    wuq = np.asarray(inp["mla_w_uq"][0], np.float32).reshape(384, 4, 192)
    nope = wuq[:, :, 0:128].reshape(384, 512)
    rope = wuq[:, :, 128:192]
    ropep = np.concatenate([rope[:, :, 32:64], rope[:, :, 0:32]], axis=2)
    wuqx = np.concatenate([nope, rope.reshape(384, 256), ropep.reshape(384, 256)], axis=1)
    wukv = np.asarray(inp["mla_w_ukv"][0], np.float32).reshape(256, 4, 256)
    wukvx = np.concatenate([wukv[:, :, 0:128].reshape(256, 512), wukv[:, :, 128:256].reshape(256, 512)], axis=1)
    sgu_w = np.asarray(inp["sgu_w"][0], np.float32)
    sgu_wT = np.ascontiguousarray(sgu_w.transpose(2, 0, 1))
    return {
        "cols": cols, "consts": consts,
        "fin_norm": f(inp["final_norm"]).reshape(1, D),
        "sgu_bias": f(inp["sgu_b"][0]).reshape(1, 512),
        "sgu_wT": sgu_wT,
        "w_in0": f(inp["w_in_even"][0]), "w_out0": f(inp["w_out_even"][0]),
        "w_in1": f(w_in1x), "w_out1": f(inp["w_out_odd"][0]),
        "wuq": f(wuqx), "wukv": f(wukvx),
        "wg0": f(inp["ffn_gate"][0]), "wg1": f(inp["ffn_gate"][1]),
        "wu0": f(inp["ffn_up"][0]), "wu1": f(inp["ffn_up"][1]),
        "wd0": f(inp["ffn_down"][0]), "wd1": f(inp["ffn_down"][1]),
    }


_PROG = {}


def kernel(**inputs):
    x = np.asarray(inputs["x"], np.float32)
    pos = np.asarray(inputs["positions"], np.int32)
    shared = _host_shared(inputs)
    if "full" not in _PROG:
        _PROG["full"] = build_program()[0]
    nc = _PROG["full"]
    in_maps = []
    for c in range(NCORES):
        m = dict(shared)
        m["x"] = np.ascontiguousarray(x[2 * c:2 * c + 2].reshape(2 * SEQ, D))
        m["pos"] = np.ascontiguousarray(pos[2 * c:2 * c + 2])
        in_maps.append(m)
    res = run_bass_kernel_spmd(nc, in_maps, core_ids=list(range(NCORES)))
    out = np.stack([np.asarray(r["out"], np.float32).reshape(2, SEQ, D) for r in res.results], axis=0)
    return out.reshape(16, SEQ, D)
```

```python
import numpy as np
from contextlib import ExitStack
import concourse.bass as bass
import concourse.mybir as mybir
from concourse.bass_utils import run_bass_kernel_spmd

F32 = mybir.dt.float32
BF16 = mybir.dt.bfloat16
I32 = mybir.dt.int32
AF = mybir.ActivationFunctionType
ALU = mybir.AluOpType
AX = mybir.AxisListType

NCORES = 8
SEQ = 2048
D = 1024
DFF = 2816
T = 512
NSUB = T // 128
NTILE = SEQ // T
EPS = 1e-6
ATT_SCALE = 192.0 ** -0.5
STRICT = True


class Buf:
    __slots__ = ("name", "w", "r", "children")

    def __init__(self, name, children=()):
        self.name = name
        self.w = {}
        self.r = {}
        self.children = list(children)


class Op:
    __slots__ = ("eng", "fn", "deps", "signal", "seq", "chan", "val")

    def __init__(self, eng, fn):
        self.eng = eng
        self.fn = fn
        self.deps = []
        self.signal = False
        self.seq = 0
        self.chan = None
        self.val = 0


class Sched:
    ENGS = ("pe", "act", "dve", "pool", "sp")

    def __init__(self, nc, es):
        self.nc = nc
        self.es = es
        self.streams = {e: [] for e in self.ENGS}
        self.sems = {}
        self.chan_cnt = {}
        for e in ("pe", "act", "dve", "pool"):
            self.sems[e] = es.enter_context(nc.semaphore("s_" + e))
        self.out_chans = set()

    def _chan(self, chan):
        if chan not in self.sems:
            self.sems[chan] = self.es.enter_context(self.nc.semaphore("c_" + str(chan)))
            self.chan_cnt[chan] = 0
        return self.sems[chan]

    def _dep(self, op, key, ref):
        if not isinstance(ref, int):
            if ref.eng == op.eng and op.eng == "pe":
                return
            ref.signal = True
        op.deps.append((key, ref))

    @staticmethod
    def _expand(bufs):
        out = []
        for b in bufs:
            out.append(b)
            out.extend(b.children)
        return out

    def _track(self, op, key, ref, reads, writes, compute):
        reads = self._expand(reads)
        writes = self._expand(writes)
        for b in reads:
            for k, r in b.w.items():
                self._dep(op, k, r)
        for b in writes:
            for k, r in b.w.items():
                if compute and k == op.eng and not STRICT:
                    continue
                self._dep(op, k, r)
            for k, r in b.r.items():
                if compute and k == op.eng and not STRICT:
                    continue
                self._dep(op, k, r)
        for b in reads:
            b.r[key] = ref
        for b in writes:
            b.w = {key: ref}
            b.r = {}

    def op(self, eng, fn, reads=(), writes=()):
        o = Op(eng, fn)
        self._track(o, eng, o, reads, writes, True)
        self.streams[eng].append(o)
        return o

    def dma(self, q, out, in_, reads=(), writes=(), chan=None, is_output=False):
        self._chan(chan)
        o = Op(q, lambda e: e.dma_start(out=out, in_=in_))
        o.chan = chan
        self.chan_cnt[chan] += 16
        o.val = self.chan_cnt[chan]
        self._track(o, chan, o.val, reads, writes, False)
        self.streams[q].append(o)
        if is_output:
            self.out_chans.add(chan)
        return o

    def finalize(self):
        nc = self.nc
        fin = Op("sp", None)
        for ch in sorted(self.out_chans, key=str):
            fin.deps.append((ch, self.chan_cnt[ch]))
        self.streams["sp"].append(fin)
        for e in ("pe", "act", "dve", "pool"):
            n = 0
            for o in self.streams[e]:
                if o.chan is None and o.signal:
                    n += 1
                    o.seq = n
        sems = self.sems
        streams = self.streams
        stats = {}

        def mk(ename):
            def body(eng):
                seen = {}
                nw = 0
                for o in streams[ename]:
                    for key, ref in o.deps:
                        v = ref if isinstance(ref, int) else ref.seq
                        assert v > 0, (ename, key)
                        if seen.get(key, 0) >= v:
                            continue
                        seen[key] = v
                        eng.wait_ge(sems[key], v)
                        nw += 1
                    if o.fn is None:
                        continue
                    ins = o.fn(eng)
                    if o.chan is not None:
                        ins.then_inc(sems[o.chan], 16)
                    elif o.signal:
                        ins.then_inc(sems[ename], 1)
                stats[ename] = (len(streams[ename]), nw)
            return body

        with nc.Block() as block:
            block.tensor(mk("pe"))
            block.scalar(mk("act"))
            block.vector(mk("dve"))
            block.gpsimd(mk("pool"))
            block.sync(mk("sp"))
        return stats


def _col_layout():
    lay = {}
    off = 0

    def add(name, n):
        nonlocal off
        lay[name] = (off, n)
        off += n

    for nm in ("mixn0", "mixn1", "ffnn0", "ffnn1"):
        add(nm, 8)
    add("l0", 4)
    add("l1", 4)
    add("gnorm", 4)
    add("convb", 4)
    add("clng", 4)
    add("clnb", 4)
    add("qnorm", 3)
    add("kvnorm", 2)
    add("sgulg", 4)
    add("sgulb", 4)
    add("convw", 4 * 31)
    add("invs", 1)
    add("sgn2pi", 1)
    add("eps", 1)
    add("zero", 1)
    add("quart", 1)
    return lay, off


COLS, NCOL = _col_layout()


def _fm(v, nchunk):
    return np.ascontiguousarray(np.asarray(v, np.float32).reshape(nchunk, 128).T)


def build_program(layers=(0, 1), ntiles_total=2 * NTILE, taps=None, final_norm=True):
    taps = taps or {}
    nc = bass.Bass("TRN2", target_bir_lowering=False)

    def din(name, shape, dt=F32):
        return nc.dram_tensor(name, list(shape), dt, kind="ExternalInput").ap()

    x_d = din("x", [2 * SEQ, D])
    pos_d = din("pos", [2, SEQ], I32)
    cols_d = din("cols", [128, NCOL])
    consts_d = din("consts", [128, 3, 128])
    finbc_d = din("fin_norm", [1, D])
    sgubias_d = din("sgu_bias", [1, 512])
    sguw_d = din("sgu_wT", [128, 4, 128])
    w_in0_d = din("w_in0", [D, 3072])
    w_out0_d = din("w_out0", [D, D])
    w_in1_d = din("w_in1", [D, 1792])
    w_out1_d = din("w_out1", [D, D])
    wuq_d = din("wuq", [384, 1024])
    wukv_d = din("wukv", [256, 1024])
    wg_d = [din("wg0", [D, DFF]), din("wg1", [D, DFF])]
    wu_d = [din("wu0", [D, DFF]), din("wu1", [D, DFF])]
    wd_d = [din("wd0", [DFF, D]), din("wd1", [DFF, D])]
    out_d = nc.dram_tensor("out", [2 * SEQ, D], F32, kind="ExternalOutput").ap()
    tap_d = {}
    for name, (shape, dt) in taps.items():
        tap_d[name] = nc.dram_tensor("tap_" + name, list(shape), dt, kind="ExternalOutput").ap()

    es = ExitStack()
    with es:
        S = Sched(nc, es)

        def sb(name, shape, dt):
            return es.enter_context(nc.sbuf_tensor("sb_" + name, list(shape), dt))

        def MM(out, lhsT, rhs, st, sp, R, W):
            S.op("pe", lambda e: e.matmul(out, lhsT=lhsT, rhs=rhs, start=st, stop=sp, skip_group_check=True), R, W)

        def TR(out, in_, ident, R, W):
            S.op("pe", lambda e: e.transpose(out=out, in_=in_, identity=ident), R, W)

        def ACT(out, in_, func, R, W, **kw):
            S.op("act", lambda e: e.activation(out=out, in_=in_, func=func, **kw), R, W)

        def TT(out, in0, in1, op, R, W, eng="dve"):
            S.op(eng, lambda e: e.tensor_tensor(out=out, in0=in0, in1=in1, op=op), R, W)

        def TS(out, in0, s1, op0, R, W, s2=None, op1=None, eng="dve"):
            if op1 is None:
                S.op(eng, lambda e: e.tensor_scalar(out=out, in0=in0, scalar1=s1, scalar2=None, op0=op0), R, W)
            else:
                S.op(eng, lambda e: e.tensor_scalar(out=out, in0=in0, scalar1=s1, scalar2=s2, op0=op0, op1=op1), R, W)

        def STT(out, in0, scalar, in1, op0, op1, R, W, eng="dve"):
            S.op(eng, lambda e: e.scalar_tensor_tensor(out=out, in0=in0, scalar=scalar, in1=in1, op0=op0, op1=op1), R, W)

        def CP(out, in_, R, W, eng="dve"):
            S.op(eng, lambda e: e.tensor_copy(out=out, in_=in_), R, W)

        def MEMSET(ap, val, W, eng="dve"):
            S.op(eng, lambda e: e.memset(ap, val), (), W)

        def SCAN(out, d0, d1, R, W):
            S.op("dve", lambda e: e.tensor_tensor_scan(out=out, data0=d0, data1=d1, initial=0.0, op0=ALU.mult, op1=ALU.add), R, W)

        def RSUM(out, in_, R, W):
            S.op("dve", lambda e: e.reduce_sum(out=out, in_=in_, axis=AX.X), R, W)

        def RECIP(out, in_, R, W):
            S.op("dve", lambda e: e.reciprocal(out=out, in_=in_), R, W)

        def tap(name, ap, buf):
            if name in tap_d:
                S.dma("sp", tap_d[name], ap, reads=[buf], chan="tap", is_output=True)

        psum = [es.enter_context(nc.psum_tensor("ps%d" % i, [128, 512], F32)) for i in range(8)]
        Bps = [Buf("ps%d" % i) for i in range(8)]
        ring_state = {"i": 0}

        def pp():
            i = ring_state["i"]
            ring_state["i"] = (i + 1) % 4
            return psum[i], Bps[i]

        def named(i):
            return psum[4 + i], Bps[4 + i]

        cols = sb("cols", [128, NCOL], F32); Bcols = Buf("cols")

        def col(name, i=0, n=1, p=128):
            o, _ = COLS[name]
            return cols[0:p, o + i:o + i + n]

        consts = sb("consts", [128, 3, 128], BF16); Bconst = Buf("consts")
        ident = consts[:, 0, :]
        maskHG = consts[:, 1, :]
        maskAT = consts[:, 2, :]
        onesb = sb("onesb", [128, 128], BF16)
        rmask = sb("rmask", [128, T], F32)
        finbc = sb("finbc", [128, D], F32)
        sgubias = sb("sgubias", [128, 512], F32)
        sguw = sb("sguw", [128, 4, 128], BF16)
        wuq = sb("wuq", [128, 3, 1024], BF16)
        wukv = sb("wukv", [128, 2, 1024], BF16)
        lbc = sb("lbc", [128, 8], F32)
        Bres = Buf("resident")

        xr = sb("xr", [128, NSUB, D], F32); Bxr = [Buf("xr%d" % i) for i in range(NSUB)]
        hT = sb("hT", [128, 8, T], BF16); BhT = [Buf("hT0"), Buf("hT1")]
        Fb = [sb("F%d" % i, [128, 4, T], F32) for i in range(3)]
        BFh = [[Buf("F%d_%d" % (i, h)) for h in range(4)] for i in range(3)]
        BF = [Buf("F%d" % i, BFh[i]) for i in range(3)]
        Hb = [sb("H%d" % i, [128, 4, T], BF16) for i in range(6)]
        BH = [Buf("H%d" % i) for i in range(6)]
        mixT = sb("mixT", [128, 8, T], BF16); Bmix = [Buf("mix%d" % i) for i in range(8)]
        Kn = sb("Kn", [128, 4, SEQ], BF16); BKn = Buf("Kn")
        kr = sb("kr", [64, SEQ], BF16); Bkr = Buf("kr")
        Vc = sb("Vc", [128, SEQ // 128, 512], BF16); BV = Buf("Vc")
        HG = sb("HG", [128, 4, 32 + T], BF16); BHG = Buf("HG")
        Mb = [sb("M%d" % i, [128, T], F32) for i in range(4)]
        BM = [Buf("M%d" % i) for i in range(4)]
        CS = sb("CS", [64, T], F32); SN = sb("SN", [64, T], F32); Brope = Buf("rope")
        Sst = sb("Sst", [128, 4, 128], F32); BS = [Buf("S%d" % i) for i in range(4)]
        hs = [sb("hs%d" % i, [128, D], BF16) for i in range(3)]; Bhs = [Buf("hs%d" % i) for i in range(3)]
        nsm = sb("nsm", [128, 8], F32); Bn = [Buf("n%d" % i) for i in range(NSUB)]
        nsf = sb("nsf", [128, 8], F32); Bnf = [Buf("nf%d" % i) for i in range(NSUB)]
        iki = sb("iki", [64, T], I32); Biki = Buf("iki")
        Bsmh = [Buf("small_h%d" % h) for h in range(4)]
        small = sb("small", [128, 256], F32); Bsmall = Buf("small", Bsmh)
        NWS = 4
        wring = [sb("wr%d" % i, [128, 8, 512], BF16) for i in range(NWS)]
        Bw = [Buf("wr%d" % i) for i in range(NWS)]

        def mkring(name, n, shape, dt):
            ts = [sb("%s%d" % (name, i), shape, dt) for i in range(n)]
            bs = [Buf("%s%d" % (name, i)) for i in range(n)]
            st = {"i": 0}

            def nxt():
                i = st["i"]
                st["i"] = (i + 1) % n
                return ts[i], bs[i]
            return nxt

        r_bf128 = mkring("rb", 8, [128, 128], BF16)
        r_bf512 = mkring("re", 4, [128, T], BF16)
        r_f512 = mkring("rz", 2, [128, T], F32)

        xstage = [mixT[:, 0:4, :].rearrange("p a b -> p (a b)").bitcast(F32),
                  mixT[:, 4:8, :].rearrange("p a b -> p (a b)").bitcast(F32),
                  Hb[4][:].rearrange("p a b -> p (a b)").bitcast(F32),
                  Hb[5][:].rearrange("p a b -> p (a b)").bitcast(F32)]
        xstage_b = [Bmix[0:4], Bmix[4:8], [BH[4]], [BH[5]]]

        wq = {"blocks": [], "issued": 0, "taken": 0, "released": 0}

        def wplan(dram, r0, nk, c0, ncols):
            wq["blocks"].append((dram, r0, nk, c0, ncols))

        def _wissue(i):
            dram, r0, nk, c0, ncols = wq["blocks"][i]
            slot = i % NWS
            src = dram[r0:r0 + nk * 128, c0:c0 + ncols].rearrange("(k p) n -> p k n", p=128)
            S.dma("pool", wring[slot][:, 0:nk, 0:ncols], src, writes=[Bw[slot]], chan="w%d" % slot)

        def wget(n=1):
            wq["released"] = wq["taken"]
            while wq["issued"] < min(len(wq["blocks"]), wq["released"] + NWS):
                _wissue(wq["issued"])
                wq["issued"] += 1
            res = []
            for _ in range(n):
                i = wq["taken"]
                assert i < wq["issued"]
                wq["taken"] += 1
                res.append((wring[i % NWS], Bw[i % NWS]))
            return res[0] if n == 1 else res

        def plan_tile():
            if 0 in layers:
                for c0 in (512, 1024, 0, 1536, 2560, 2048):
                    wplan(w_in0_d, 0, 8, c0, 512)
                for cb in range(2):
                    wplan(w_out0_d, 0, 8, cb * 512, 512)
                plan_ffn(0)
            if 1 in layers:
                wplan(w_in1_d, 0, 8, 0, 512)
                wplan(w_in1_d, 0, 8, 512, 512)
                wplan(w_in1_d, 0, 8, 1024, 384)
                wplan(w_in1_d, 0, 8, 1408, 384)
                for cb in range(2):
                    wplan(w_out1_d, 0, 8, cb * 512, 512)
                plan_ffn(1)

        def plan_ffn(l):
            for cb in range(6):
                ncw = 512 if cb < 5 else 256
                wplan(wg_d[l], 0, 8, cb * 512, ncw)
                wplan(wu_d[l], 0, 8, cb * 512, ncw)
            for cb in range(2):
                for kg, nk in enumerate((8, 8, 6)):
                    wplan(wd_d[l], kg * 1024, nk, cb * 512, 512)

        for _ in range(ntiles_total):
            plan_tile()

        S.dma("sp", cols[:], cols_d, writes=[Bcols], chan="c0")
        S.dma("pool", consts[:], consts_d, writes=[Bconst], chan="c1")
        S.dma("sp", finbc[:], finbc_d.partition_broadcast(128), writes=[Bres], chan="c2")
        S.dma("sp", sgubias[:], sgubias_d.partition_broadcast(128), writes=[Bres], chan="c2")
        S.dma("pool", sguw[:], sguw_d, writes=[Bres], chan="c3")
        S.dma("pool", wuq[:], wuq_d.rearrange("(k p) n -> p k n", p=128), writes=[Bres], chan="c3")
        S.dma("pool", wukv[:], wukv_d.rearrange("(k p) n -> p k n", p=128), writes=[Bres], chan="c3")
        MEMSET(onesb[:], 1.0, [Bres])
        MEMSET(rmask[:], 1.0, [Bres])
        MEMSET(rmask[:].rearrange("p (c t) -> p c t", t=64)[:, :, 0], 0.0, [Bres])
        TT(sguw[:], sguw[:], consts[:, 2:3, :].to_broadcast([128, 4, 128]), ALU.mult, [Bres, Bconst], [Bres])
        pw, pwb = pp()
        MM(pw[:, :], onesb[:], sguw[:].rearrange("p g t -> p (g t)"), True, True, [Bres], [pwb])
        for g in range(4):
            STT(sgubias[:, g * 128:(g + 1) * 128], pw[:, g * 128:(g + 1) * 128], col("sgulb", g), sgubias[:, g * 128:(g + 1) * 128],
                ALU.mult, ALU.add, [pwb, Bcols, Bres], [Bres])
        TT(lbc[:, 0:4], col("l0", 0, 4), col("l1", 0, 4), ALU.subtract, [Bcols], [Bres])
        ACT(lbc[:, 4:8], lbc[:, 0:4], AF.Sigmoid, [Bres], [Bres], scale=-1.0)
        ACT(lbc[:, 0:4], lbc[:, 0:4], AF.Sigmoid, [Bres], [Bres])

        def rms_rstd(ssq_ap, out_ap, n, inv_n, R, W):
            ACT(out_ap, ssq_ap, AF.Ln, R, W, scale=inv_n, bias=col("eps", p=n))
            ACT(out_ap, out_ap, AF.Exp, W, W, scale=-0.5)

        def xsrc(sub, staged):
            return (xstage[sub], xstage_b[sub]) if staged else (xr[:, sub, :], [Bxr[sub]])

        def norm_pre1(sub, staged=False):
            h_t, h_b = hs[sub % 3], Bhs[sub % 3]
            xa, xb = xsrc(sub, staged)
            MEMSET(nsm[:, sub:sub + 1], 0.0, [Bn[sub]])
            ACT(h_t[:], xa, AF.Square, xb, [h_b, Bn[sub]], accum_out=nsm[:, sub:sub + 1])
            ACT(nsm[:, 4 + sub:5 + sub], nsm[:, sub:sub + 1], AF.Ln, [Bn[sub], Bcols], [Bn[sub]], scale=1.0 / D, bias=col("eps"))
            ACT(nsm[:, 4 + sub:5 + sub], nsm[:, 4 + sub:5 + sub], AF.Exp, [Bn[sub]], [Bn[sub]], scale=-0.5)

        def norm_pre2(sub, staged=False):
            h_t, h_b = hs[sub % 3], Bhs[sub % 3]
            xa, xb = xsrc(sub, staged)
            if sub % 2 == 0:
                ACT(h_t[:], xa, AF.Copy, xb + [Bn[sub]], [h_b], scale=nsm[:, 4 + sub:5 + sub])
            else:
                TS(h_t[:], xa, nsm[:, 4 + sub:5 + sub], ALU.mult, xb + [Bn[sub]], [h_b])

        def norm_post(sub, normname):
            h_t, h_b = hs[sub % 3], Bhs[sub % 3]
            pt, pb = pp()
            ptb = pt[:].bitcast(BF16)
            for c in range(8):
                TR(ptb[:, c * 128:(c + 1) * 128], h_t[:, c * 128:(c + 1) * 128], ident, [h_b, Bconst], [pb])
            o, _ = COLS[normname]
            TT(hT[:, :, sub * 128:(sub + 1) * 128], ptb.rearrange("p (c t) -> p c t", c=8),
               cols[:, o:o + 8].unsqueeze(2).to_broadcast([128, 8, 128]), ALU.mult, [pb, Bcols], [BhT[sub // 2]])

        def norm_hook(normname, staged=False):
            def after(sub):
                norm_pre1(sub, staged)
                if sub >= 1:
                    norm_pre2(sub - 1, staged)
                if sub >= 2:
                    norm_post(sub - 2, normname)
                if sub == NSUB - 1:
                    norm_pre2(sub, staged)
                    norm_post(sub - 1, normname)
                    norm_post(sub, normname)
            return after

        def norm_to_hT(normname, staged=False):
            hk = norm_hook(normname, staged)
            for sub in range(NSUB):
                hk(sub)

        def proj_fm(wt, wb, c0, m, nk=8, mcols=128, rhs=None, rhsb=None, nrows=128, halves=False):
            pt, pb = pp()
            if rhs is None:
                if halves:
                    for hf in range(2):
                        csl = slice(hf * 256, (hf + 1) * 256)
                        for k in range(nk):
                            MM(pt[0:mcols, csl], wt[0:nrows, k, c0 + m * 128:c0 + m * 128 + mcols], hT[0:nrows, k, csl], k == 0, k == nk - 1, [wb, BhT[hf]], [pb])
                    return pt, pb
                rhs, rb_ = hT, BhT
            else:
                rb_ = [rhsb]
            for k in range(nk):
                MM(pt[0:mcols, :], wt[0:nrows, k, c0 + m * 128:c0 + m * 128 + mcols], rhs[0:nrows, k, :], k == 0, k == nk - 1, [wb] + rb_, [pb])
            return pt, pb

        def proj_tm(wt, wb, c0, ncols, sub, nk=8, lhs=None, lhsb=None):
            if lhs is None:
                lhs, lhsb = hT, BhT[sub // 2]
            pt, pb = pp()
            for k in range(nk):
                MM(pt[:, 0:ncols], lhs[:, k, sub * 128:(sub + 1) * 128], wt[:, k, c0:c0 + ncols], k == 0, k == nk - 1, [wb, lhsb], [pb])
            return pt, pb

        def out_proj(after=None):
            wts = wget(2)
            for sub in range(NSUB):
                for cb in range(2):
                    wt, wb = wts[cb]
                    pt, pb = pp()
                    for k in range(8):
                        MM(pt[:, :], mixT[:, k, sub * 128:(sub + 1) * 128], wt[:, k, 0:512], k == 0, k == 7, [wb, Bmix[k]], [pb])
                    TT(xr[:, sub, cb * 512:(cb + 1) * 512], xr[:, sub, cb * 512:(cb + 1) * 512], pt[:, :], ALU.add, [pb, Bxr[sub]], [Bxr[sub]])
                if after is not None:
                    after(sub)

        def ffn(l, after=None, prefetch=None):
            for cb in range(6):
                if cb == 1 and prefetch is not None:
                    prefetch()
                nm = 4 if cb < 5 else 2
                (wgt, wgb), (wut, wub) = wget(2)
                for m in range(nm):
                    pg, pgb = proj_fm(wgt, wgb, 0, m, halves=(cb == 0 and m == 0))
                    pu, pub = proj_fm(wut, wub, 0, m)
                    sl, slb = r_bf512()
                    ACT(sl[:], pg[:, :], AF.Silu, [pgb], [slb])
                    ch = cb * 4 + m
                    TT(actv(ch), sl[:], pu[:, :], ALU.mult, [slb, pub], actb(ch))
            accs = [named(i) for i in range(4)]
            for kg, nk in enumerate((8, 8, 6)):
                wt, wb = wget()
                for sub in range(NSUB):
                    at, ab = accs[sub]
                    for k in range(nk):
                        ch = kg * 8 + k
                        MM(at[:, :], actv(ch)[:, sub * 128:(sub + 1) * 128], wt[:, k, 0:512], ch == 0, ch == 21, [wb] + actb(ch), [ab])
                    if kg == 2:
                        TT(xr[:, sub, 0:512], xr[:, sub, 0:512], at[:, :], ALU.add, [ab, Bxr[sub]], [Bxr[sub]])
            wt, wb = wget()
            for sub in range(NSUB):
                at, ab = accs[sub]
                for k in range(8):
                    MM(at[:, :], actv(k)[:, sub * 128:(sub + 1) * 128], wt[:, k, 0:512], k == 0, False, [wb] + actb(k), [ab])
            wts = wget(2)
            for sub in range(NSUB):
                at, ab = accs[sub]
                for kg, nk in ((1, 8), (2, 6)):
                    wt, wb = wts[kg - 1]
                    for k in range(nk):
                        ch = kg * 8 + k
                        MM(at[:, :], actv(ch)[:, sub * 128:(sub + 1) * 128], wt[:, k, 0:512], False, ch == 21, [wb] + actb(ch), [ab])
                TT(xr[:, sub, 512:1024], xr[:, sub, 512:1024], at[:, :], ALU.add, [ab, Bxr[sub]], [Bxr[sub]])
                if after is not None:
                    after(sub)

        Fbf = [Fb[i][:].rearrange("p a b -> p (a b)").bitcast(BF16) for i in range(3)]

        def actv(ch):
            return Fbf[ch // 8][:, (ch % 8) * T:(ch % 8 + 1) * T]

        def actb(ch):
            return [BF[ch // 8]]

        def layer0(first_in_seq, ffn_after, ffn_prefetch=None):
            F1, F2, F3 = Fb
            B1, B2, B3 = BF
            if first_in_seq:
                MEMSET(Sst[:], 0.0, BS)
            norm_to_hT("mixn0", staged=True)
            tap("hT0", hT[:], BhT[1])
            wt, wb = wget()
            for h in range(4):
                pt, pb = proj_fm(wt, wb, 0, h, halves=(h == 0))
                ACT(F1[:, h, :], pt[:, :], AF.Sigmoid, [pb], [B1])
            (wti, wbi), (wtq, wbq) = wget(2)
            B1h, B2h, B3h = BFh
            for h in range(4):
                TS(F2[:, h, :], F1[:, h, :], -1.0, ALU.mult, [B1h[h]], [B2h[h]], s2=1.0, op1=ALU.add)
                ACT(F1[:, h, :], F1[:, h, :], AF.Ln, [B1h[h], Bres], [B1h[h]], scale=lbc[:, 4 + h:5 + h], bias=lbc[:, h:h + 1])
                SCAN(F3[:, h, :], rmask[:], F1[:, h, :], [B1h[h], Bres], [B3h[h]])
                pt, pb = proj_tm(wti, wbi, 0, 512, h)
                ACT(Hb[4][:, h, :], pt[:, :], AF.Copy, [pb], [BH[4]])
                bh = F3[:, h, :].rearrange("p (c t) -> p c t", t=64)
                hs8 = slice(h * 8, (h + 1) * 8)
                ACT(small[:, 16:48][:, hs8], bh[:, :, 63], AF.Exp, [B3h[h]], [Bsmh[h]])
                ACT(small[:, 48:80][:, hs8], bh[:, :, 31], AF.Exp, [B3h[h]], [Bsmh[h]])
                CP(small[:, 80:112][:, hs8], bh[:, :, 31], [B3h[h]], [Bsmh[h]])
                TT(bh, bh, small[:, 80:112][:, hs8].unsqueeze(2).to_broadcast([128, 8, 64]), ALU.subtract, [B3h[h], Bsmh[h]], [B3h[h]])
                ACT(F1[:, h, :], F3[:, h, :], AF.Exp, [B3h[h]], [B1h[h]])
                ACT(F3[:, h, :], F3[:, h, :], AF.Exp, [B3h[h]], [B3h[h]], scale=-1.0)
                pt, pb = proj_fm(wtq, wbq, 0, h)
                TT(Hb[0][:, h, :], pt[:, :], F1[:, h, :], ALU.mult, [pb, B1h[h]], [BH[0]])
                STT(Hb[1][:, h, :], F2[:, h, :], lbc[:, 4 + h:5 + h], F3[:, h, :], ALU.mult, ALU.mult, [B2h[h], B3h[h], Bres], [BH[1]])
                TT(Hb[3][:, h, :].rearrange("p (c t) -> p c t", t=64), Hb[1][:, h, :].rearrange("p (c t) -> p c t", t=64),
                   F1[:, h, :].rearrange("p (c t) -> p c t", t=64)[:, :, 63:64].to_broadcast([128, 8, 64]), ALU.mult, [BH[1], B1h[h]], [BH[3]])
            wtg, wbg = wget()
            for h in range(4):
                pt, pb = proj_fm(wtg, wbg, 0, h)
                ACT(Hb[2][:, h, :], pt[:, :], AF.Silu, [pb], [BH[2]])
            tap("qp", Hb[0][:], BH[0])
            tap("kp", Hb[1][:], BH[1])
            for sub in range(NSUB):
                pt, pb = pp()
                ptb = pt[:].bitcast(BF16)
                for h in range(4):
                    TR(ptb[:, h * 128:(h + 1) * 128], Hb[3][:, h, sub * 128:(sub + 1) * 128], ident, [BH[3], Bconst], [pb])
                ACT(Hb[5][:, sub, :], ptb[:, 0:512], AF.Copy, [pb], [BH[5]])
            kTM, iTM, qT, kT = Hb[5], Hb[4], Hb[0], Hb[1]
            pO = [named(h) for h in range(4)]
            TUv = [F2[:].rearrange("p a (b v) -> p (a b) v", v=128), F3[:].rearrange("p a (b v) -> p (a b) v", v=128)]
            Btu = [B2, B3]
            SBv = [Hb[1][:].rearrange("p a (b v) -> p (a b) v", v=128), Hb[5][:].rearrange("p a (b v) -> p (a b) v", v=128)]
            Bsb = [BH[1], BH[5]]
            E1v = small[:, 16:48].rearrange("p (h c) -> p h c", c=8)
            E3v = small[:, 48:80].rearrange("p (h c) -> p h c", c=8)

            def TU(c):
                return TUv[c // 4][:, (c % 4) * 4:(c % 4) * 4 + 4, :]

            def SBF(c):
                return SBv[c // 4][:, (c % 4) * 4:(c % 4) * 4 + 4, :]

            for sub in range(NSUB):
                tsl = slice(sub * 128, (sub + 1) * 128)
                psS, psSb = pp()
                for h in range(4):
                    hsl = slice(h * 128, (h + 1) * 128)
                    MM(psS[:, hsl], kT[:, h, tsl], qT[:, h, tsl], True, True, [BH[1], BH[0]], [psSb])
                TT(mixT[:, 4 + sub, :].rearrange("p (h t) -> p h t", h=4), psS[:, :].rearrange("p (h t) -> p h t", h=4),
                   consts[:, 1:2, :].to_broadcast([128, 4, 128]), ALU.mult, [psSb, Bconst], [Bmix[4 + sub]])
                for cc in range(2):
                    c = sub * 2 + cc
                    ps_ = slice(cc * 64, cc * 64 + 64)
                    pu, pub = pp()
                    for h in range(4):
                        hsl = slice(h * 128, (h + 1) * 128)
                        MM(pu[:, hsl], kTM[ps_, sub, hsl], iTM[ps_, sub, hsl], True, True, [BH[5], BH[4]], [pub])
                    ACT(TU(c), pu[:, :].rearrange("p (h v) -> p h v", h=4), AF.Copy, [pub], [Btu[c // 4]])
            for c in range(T // 64):
                TT(SBF(c), Sst[:], E3v[:, :, c].unsqueeze(2).to_broadcast([128, 4, 128]), ALU.mult, BS + [Bsmall], [Bsb[c // 4]])
                for h in range(4):
                    STT(Sst[:, h, :], Sst[:, h, :], E1v[:, h, c:c + 1], TU(c)[:, h, :], ALU.mult, ALU.add, [BS[h], Bsmall, Btu[c // 4]], [BS[h]])
            wt, wb = wget()
            for sub in range(NSUB):
                pt, pb = proj_tm(wt, wb, 0, 512, sub)
                ACT(F1[:, sub, :], pt[:, :], AF.Gelu_apprx_tanh, [pb], [B1])
            wt, wb = wget()
            for g in range(4):
                pt, pb = proj_fm(wt, wb, 0, g)
                ACT(Hb[3][:, g, :], pt[:, :], AF.Gelu_apprx_tanh, [pb], [BH[3]])
            for sub in range(NSUB):
                tsl = slice(sub * 128, (sub + 1) * 128)
                for h in range(4):
                    hsl = slice(h * 128, (h + 1) * 128)
                    po, pob = pO[h]
                    MM(po[:, tsl], iTM[:, sub, hsl], mixT[:, 4 + sub, hsl], True, False, [BH[4], Bmix[4 + sub]], [pob])
                    for cc in range(2):
                        c = sub * 2 + cc
                        t0 = sub * 128 + cc * 64
                        MM(po[:, t0:t0 + 64], SBF(c)[:, h, :], qT[:, h, t0:t0 + 64], False, cc == 1, [Bsb[c // 4], BH[0]], [pob])
            osqs = []
            for h in range(4):
                po, pob = pO[h]
                osq, osqb = r_bf512()
                ACT(osq[:], po[:, :], AF.Square, [pob], [osqb])
                osqs.append((osq, osqb))
            pns = []
            for h in range(4):
                osq, osqb = osqs[h]
                pn, pnb = pp()
                MM(pn[:, :], onesb[:], osq[:], True, True, [Bres, osqb], [pnb])
                pns.append((pn, pnb))
            r4 = [(Mb[i], BM[i]) for i in range(4)]
            for h in range(4):
                pn, pnb = pns[h]
                rms_rstd(pn[:, :], r4[h][0][:], 128, 1.0 / 128, [pnb, Bcols], [r4[h][1]])
            v16 = F1[:].rearrange("p s (g c) -> p (s g) c", c=128)
            sums = small[:, 112:128]
            sumsq = small[:, 128:144]
            mean = small[:, 144:160]
            rstd = small[:, 160:176]
            RSUM(sums, v16, [B1], [Bsmall])
            TT(F2[:], F1[:], F1[:], ALU.mult, [B1], [B2])
            RSUM(sumsq, F2[:].rearrange("p s (g c) -> p (s g) c", c=128), [B2], [Bsmall])
            TS(mean, sums, 1.0 / 128, ALU.mult, [Bsmall], [Bsmall])
            TT(v16, v16, mean.unsqueeze(2).to_broadcast([128, 16, 128]), ALU.subtract, [B1, Bsmall], [B1])
            TT(sums, mean, mean, ALU.mult, [Bsmall], [Bsmall])
            STT(sumsq, sumsq, 1.0 / 128, sums, ALU.mult, ALU.subtract, [Bsmall], [Bsmall])
            rms_rstd(sumsq, rstd, 128, 1.0, [Bsmall, Bcols], [Bsmall])
            TT(Hb[4][:].rearrange("p s (g c) -> p (s g) c", c=128), v16, rstd.unsqueeze(2).to_broadcast([128, 16, 128]), ALU.mult, [B1, Bsmall], [BH[4]])
            vn = Hb[4]
            for g in range(4):
                pz, pzb = pp()
                for sub in range(NSUB):
                    MM(pz[:, sub * 128:(sub + 1) * 128], vn[:, sub, g * 128:(g + 1) * 128], sguw[:, g, :], True, True, [BH[4], Bres], [pzb])
                zt, ztb = r_f512()
                STT(zt[:].rearrange("p (s t) -> p s t", s=4), pz[:, :].rearrange("p (s t) -> p s t", s=4), col("sgulg", g),
                    sgubias[:, g * 128:(g + 1) * 128].unsqueeze(1).to_broadcast([128, 4, 128]), ALU.mult, ALU.add, [pzb, Bres, Bcols], [ztb])
                TT(mixT[:, 4 + g, :], zt[:], Hb[3][:, g, :], ALU.mult, [ztb, BH[3]], [Bmix[4 + g]])
            for h in range(4):
                po, pob = pO[h]
                tmpo, tmpob = r_f512()
                TT(tmpo[:], po[:, :], r4[h][0][:], ALU.mult, [pob, r4[h][1]], [tmpob])
                STT(mixT[:, h, :], tmpo[:], col("gnorm", h), Hb[2][:, h, :], ALU.mult, ALU.mult, [tmpob, Bcols, BH[2]], [Bmix[h]])
            tap("aout", mixT[:, 0:4, :], Bmix[3])
            tap("bout", mixT[:, 4:8, :], Bmix[7])
            out_proj(after=norm_hook("ffnn0"))
            tap("x_mix0", xr[:], Bxr[3])
            ffn(0, after=ffn_after, prefetch=ffn_prefetch)

        def layer1(seq, j, first_in_seq, pre_normed, ffn_after, ffn_prefetch=None):
            F1, F2, F3 = Fb
            B1, B2, B3 = BF
            tok0 = j * T
            S.dma("sp", iki[:], pos_d[seq:seq + 1, tok0:tok0 + T].partition_broadcast(64), writes=[Biki], chan="pos")
            if not pre_normed:
                norm_to_hT("mixn1", staged=True)
            if first_in_seq:
                MEMSET(HG[:, :, 0:32], 0.0, [BHG])
            wt, wb = wget()
            for c in range(4):
                pt, pb = proj_fm(wt, wb, 0, c, halves=(c == 0))
                ACT(F2[:, c, :], pt[:, :], AF.Copy, [pb], [B2])
            wt, wb = wget()
            for c in range(4):
                pt, pb = proj_fm(wt, wb, 0, c)
                ACT(F1[:, c, :], pt[:, :], AF.Sigmoid, [pb], [B1])
                TT(HG[:, c, 32:32 + T], F2[:, c, :], F1[:, c, :], ALU.mult, [B1, B2], [BHG])
            Y1 = F3[0:64, 1, :]
            R1 = F3[0:64, 2, :]
            CP(Y1, iki[:], [Biki], [B3])
            TS(Y1, Y1, col("invs", p=64), ALU.mult, [B3, Bcols], [B3])

            def frac_to(dst):
                CP(iki[:], Y1, [B3], [Biki])
                CP(R1, iki[:], [Biki], [B3])
                TT(R1, Y1, R1, ALU.subtract, [B3], [B3])

            frac_to(R1)
            ACT(SN[:], R1, AF.Sin, [B3, Bcols], [Brope], scale=col("sgn2pi", p=64))
            TS(Y1, Y1, 0.25, ALU.add, [B3], [B3])
            frac_to(R1)
            ACT(CS[:], R1, AF.Sin, [B3], [Brope], scale=float(2 * np.pi))
            wt, wb = wget()
            for m in range(3):
                pt, pb = proj_fm(wt, wb, 0, m)
                ACT(F3[:, m, :], pt[:, :], AF.Copy, [pb, Bcols], [B3], scale=col("qnorm", m))
                ACT(Hb[2][:, m, :], pt[:, :], AF.Square, [pb], [BH[2]])
            wt, wb = wget()
            for m in range(2):
                pt, pb = proj_fm(wt, wb, 0, m)
                ACT(F1[:, m, :], pt[:, :], AF.Copy, [pb, Bcols], [B1], scale=col("kvnorm", m))
                ACT(Hb[5][:, m, :], pt[:, :], AF.Square, [pb], [BH[5]])
            pr, prb = proj_fm(wt, wb, 256, 0, mcols=64)
            prp, prpb = proj_fm(wt, wb, 320, 0, mcols=64)
            t1, t1b = r_f512()
            t2, t2b = r_f512()
            TT(t1[0:64, :], pr[0:64, :], CS[:], ALU.mult, [prb, Brope], [t1b])
            TT(t2[0:64, :], prp[0:64, :], SN[:], ALU.mult, [prpb, Brope], [t2b])
            TT(kr[:, tok0:tok0 + T], t1[0:64, :], t2[0:64, :], ALU.add, [t1b, t2b], [Bkr])
            pn, pnb = pp()
            for m in range(3):
                MM(pn[:, :], onesb[:], Hb[2][:, m, :], m == 0, m == 2, [Bres, BH[2]], [pnb])
            pk, pkb = pp()
            for m in range(2):
                MM(pk[:, :], onesb[:], Hb[5][:, m, :], m == 0, m == 1, [Bres, BH[5]], [pkb])
            rms_rstd(pn[:, :], Mb[2][:], 128, 1.0 / 384, [pnb, Bcols], [BM[2]])
            TT(Hb[3][:, 0:3, :], F3[:, 0:3, :], Mb[2][:].unsqueeze(1).to_broadcast([128, 3, T]), ALU.mult, [B3, BM[2]], [BH[3]])
            rms_rstd(pk[:, :], Mb[2][:], 128, 1.0 / 256, [pkb, Bcols], [BM[2]])
            TT(Hb[4][:, 0:2, :], F1[:, 0:2, :], Mb[2][:].unsqueeze(1).to_broadcast([128, 2, T]), ALU.mult, [B1, BM[2]], [BH[4]])
            o_cw, _ = COLS["convw"]
            for c in range(4):
                py, pyb = pp()
                for jt in range(31):
                    dg, dgb = r_bf128()
                    TS(dg[:], ident, cols[:, o_cw + c * 31 + jt:o_cw + c * 31 + jt + 1], ALU.mult, [Bconst, Bcols], [dgb])
                    MM(py[:, :], dg[:], HG[:, c, 2 + jt:2 + jt + T], jt == 0, jt == 30, [dgb, BHG], [pyb])
                ACT(F2[:, c, :], py[:, :], AF.Identity, [pyb, Bcols], [B2], bias=col("convb", c))
                ACT(Hb[0][:, c, :], py[:, :], AF.Identity, [pyb, Bcols], [BH[0]], bias=col("convb", c))
                ACT(Hb[1][:, c, :], py[:, :], AF.Square, [pyb, Bcols], [BH[1]], bias=col("convb", c))
            CP(HG[:, :, 0:32], HG[:, :, T:T + 32], [BHG], [BHG])
            pm, pmb = named(2)
            for c in range(4):
                MM(pm[:, :], onesb[:], Hb[0][:, c, :], c == 0, c == 3, [Bres, BH[0]], [pmb])
            pq, pqb = named(3)
            for c in range(4):
                MM(pq[:, :], onesb[:], Hb[1][:, c, :], c == 0, c == 3, [Bres, BH[1]], [pqb])
            cqn, ckvn = Hb[3], Hb[4]
            qnT, qrT = Hb[0], Hb[1]
            for h in range(4):
                pt, pb = proj_fm(wuq, Bres, 0, h, nk=3, rhs=cqn, rhsb=BH[3])
                ACT(qnT[:, h, :], pt[:, :], AF.Copy, [pb], [BH[0]])
                pr, prb = proj_fm(wuq, Bres, 512 + h * 64, 0, nk=3, mcols=64, rhs=cqn, rhsb=BH[3])
                prp, prpb = proj_fm(wuq, Bres, 768 + h * 64, 0, nk=3, mcols=64, rhs=cqn, rhsb=BH[3])
                t1, t1b = r_f512()
                t2, t2b = r_f512()
                TT(t1[0:64, :], pr[0:64, :], CS[:], ALU.mult, [prb, Brope], [t1b])
                TT(t2[0:64, :], prp[0:64, :], SN[:], ALU.mult, [prpb, Brope], [t2b])
                TT(qrT[0:64, h, :], t1[0:64, :], t2[0:64, :], ALU.add, [t1b, t2b], [BH[1]])
            for h in range(4):
                pt, pb = proj_fm(wukv, Bres, 0, h, nk=2, rhs=ckvn, rhsb=BH[4])
                ACT(Kn[:, h, tok0:tok0 + T], pt[:, :], AF.Copy, [pb], [BKn])
            for sub in range(NSUB):
                pt, pb = proj_tm(wukv, Bres, 512, 512, sub, nk=2, lhs=ckvn, lhsb=BH[4])
                ACT(Vc[:, j * NSUB + sub, :], pt[:, :], AF.Copy, [pb], [BV])
            def conv_ln_chain():
                ACT(Mb[0][:], pm[:, :], AF.Copy, [pmb], [BM[0]], scale=1.0 / 512)
                TT(Mb[1][:], Mb[0][:], Mb[0][:], ALU.mult, [BM[0]], [BM[1]])
                STT(Mb[1][:], pq[:, :], 1.0 / 512, Mb[1][:], ALU.mult, ALU.subtract, [pqb, BM[1]], [BM[1]])
                rms_rstd(Mb[1][:], Mb[1][:], 128, 1.0, [BM[1], Bcols], [BM[1]])
                TT(F2[:], F2[:], Mb[0][:].unsqueeze(1).to_broadcast([128, 4, T]), ALU.subtract, [B2, BM[0]], [B2])
                TT(F2[:], F2[:], Mb[1][:].unsqueeze(1).to_broadcast([128, 4, T]), ALU.mult, [B2, BM[1]], [B2])
                for c in range(4):
                    ACT(mixT[:, c, :], F2[:, c, :], AF.Silu, [B2, Bcols], [Bmix[c]], scale=col("clng", c), bias=col("clnb", c))
                tap("cout", mixT[:, 0:4, :], Bmix[3])
            nkb = j * NSUB + NSUB
            LOOK = 3
            banks = [(named(2 * (h % 2)), named(2 * (h % 2) + 1)) for h in range(4)]

            def s_stage(h, kb):
                s0 = max(0, kb - j * NSUB)
                qsl = slice(s0 * 128, T)
                ksl = slice(kb * 128, (kb + 1) * 128)
                psS, psSb = pp()
                MM(psS[:, qsl], Kn[:, h, ksl], qnT[:, h, qsl], True, False, [BKn, BH[0]], [psSb])
                MM(psS[:, qsl], kr[0:64, ksl], qrT[0:64, h, qsl], False, True, [Bkr, BH[1]], [psSb])
                ex, exb = r_bf512()
                ACT(ex[:, qsl], psS[:, qsl], AF.Exp, [psSb], [exb], scale=ATT_SCALE)
                if kb >= j * NSUB:
                    dsl = slice(s0 * 128, (s0 + 1) * 128)
                    TT(ex[:, dsl], ex[:, dsl], maskAT, ALU.mult, [exb, Bconst], [exb])
                return ex, exb, qsl

            def pv_stage(h, kb, ex, exb, qsl):
                (po, pob), (pd, pdb) = banks[h]
                MM(po[:, qsl], Vc[:, kb, h * 128:(h + 1) * 128], ex[:, qsl], kb == kb_order[0], kb == kb_order[-1], [BV, exb], [pob])
                MM(pd[:, qsl], onesb[:], ex[:, qsl], kb == kb_order[0], kb == kb_order[-1], [Bres, exb], [pdb])
                if kb == kb_order[-1]:
                    ACT(Mb[1][:], pd[:, :], AF.Ln, [pdb], [BM[1]])
                    ACT(Mb[1][:], Mb[1][:], AF.Exp, [BM[1]], [BM[1]], scale=-1.0)
                    TT(mixT[:, 4 + h, :], po[:, :], Mb[1][:], ALU.mult, [pob, BM[1]], [Bmix[4 + h]])

            kb_order = list(range(j * NSUB, nkb)) + list(range(0, j * NSUB))
            pend = []
            for h in range(4):
                if h == 1:
                    conv_ln_chain()
                for kb in kb_order:
                    pend.append(((h, kb), s_stage(h, kb)))
                    if len(pend) > LOOK:
                        it0, st0 = pend.pop(0)
                        pv_stage(*it0, *st0)
            while pend:
                it0, st0 = pend.pop(0)
                pv_stage(*it0, *st0)
            tap("dout", mixT[:, 4:8, :], Bmix[7])
            out_proj(after=norm_hook("ffnn1"))
            tap("x_mix1", xr[:], Bxr[3])
            ffn(1, after=ffn_after, prefetch=ffn_prefetch)

        xv = x_d.rearrange("(n p) d -> p n d", p=128)
        ov = out_d.rearrange("(n p) d -> p n d", p=128)
        F01 = [Fb[0], Fb[1]]
        for ti in range(ntiles_total):
            seq, j = divmod(ti, NTILE)
            first = (j == 0)
            def stage_x(t2):
                for sub in range(NSUB):
                    S.dma("sp", xstage[sub], xv[:, t2 * NSUB + sub, :], writes=xstage_b[sub], chan="x%d" % sub)

            if ti == 0:
                stage_x(0)
            for sub in range(NSUB):
                S.dma("sp", xr[:, sub, :], xstage[sub], reads=xstage_b[sub], writes=[Bxr[sub]], chan="xc%d" % sub)
            nxt = (lambda t2=ti + 1: stage_x(t2)) if ti + 1 < ntiles_total else None
            def final_pre(sub):
                h_t, h_b = hs[sub % 3], Bhs[sub % 3]
                ssq, r0 = nsm[:, sub:sub + 1], nsm[:, 4 + sub:5 + sub]
                MEMSET(ssq, 0.0, [Bn[sub]])
                ACT(h_t[:], xr[:, sub, :], AF.Square, [Bxr[sub]], [h_b, Bn[sub]], accum_out=ssq)
                ACT(r0, ssq, AF.Ln, [Bn[sub], Bcols], [Bn[sub]], scale=1.0 / D, bias=col("eps"))
                ACT(r0, r0, AF.Exp, [Bn[sub]], [Bn[sub]], scale=-0.5)

            def final_post(sub, ti=ti):
                ot = Hb[sub][:].rearrange("p a b -> p (a b)").bitcast(F32)
                STT(ot, xr[:, sub, :], nsm[:, 4 + sub:5 + sub], finbc[:], ALU.mult, ALU.mult, [Bxr[sub], Bn[sub], Bres], [BH[sub]])
                S.dma("sp", ov[:, ti * NSUB + sub, :], ot, reads=[BH[sub]], chan="o%d" % sub, is_output=True)

            def final_sub(sub):
                final_pre(sub)
                if sub >= 1:
                    final_post(sub - 1)
                if sub == NSUB - 1:
                    final_post(sub)

            def raw_sub(sub, ti=ti):
                S.dma("sp", ov[:, ti * NSUB + sub, :], xr[:, sub, :], reads=[Bxr[sub]], chan="o%d" % sub, is_output=True)

            last_after = final_sub if final_norm else raw_sub
            if 0 in layers and 1 in layers:
                layer0(first, norm_hook("mixn1"))
                layer1(seq, j, first, True, last_after, nxt)
            elif 0 in layers:
                layer0(first, last_after, nxt)
            else:
                layer1(seq, j, first, False, last_after, nxt)
        stats = S.finalize()
    return nc, stats


def _host_shared(inp):
    f = lambda a: np.ascontiguousarray(np.asarray(a, dtype=np.float32))
    cols = np.zeros((128, NCOL), np.float32)

    def put(name, arr):
        o, n = COLS[name]
        cols[:, o:o + n] = arr

    put("mixn0", _fm(inp["mix_norm"][0], 8))
    put("mixn1", _fm(inp["mix_norm"][1], 8))
    put("ffnn0", _fm(inp["ffn_norm"][0], 8))
    put("ffnn1", _fm(inp["ffn_norm"][1], 8))
    put("l0", _fm(inp["hgrn_lb_logits"][0], 4))
    put("l1", _fm(inp["hgrn_lb_logits"][1], 4))
    put("gnorm", _fm(inp["hgrn_gnorm"][0], 4))
    put("sgulg", _fm(inp["sgu_ln_g"][0], 4))
    put("sgulb", _fm(inp["sgu_ln_b"][0], 4))
    put("convb", _fm(inp["conv_b"][0], 4))
    put("clng", _fm(inp["conv_ln_g"][0], 4))
    put("clnb", _fm(inp["conv_ln_b"][0], 4))
    put("qnorm", _fm(inp["mla_q_norm"][0], 3))
    put("kvnorm", _fm(inp["mla_kv_norm"][0], 2))
    cw = np.asarray(inp["conv_w"][0], np.float32)
    put("convw", np.ascontiguousarray(cw.T.reshape(4, 128, 31).transpose(1, 0, 2).reshape(128, 124)))
    inv = 1.0 / (10000.0 ** (np.arange(0, 64, 2, dtype=np.float32) / 64.0))
    inv2 = np.concatenate([inv, inv]).astype(np.float32)
    invs = np.zeros(128, np.float32)
    invs[:64] = (inv2.astype(np.float64) / (2 * np.pi)).astype(np.float32)
    put("invs", invs[:, None])
    sg = np.zeros(128, np.float32)
    sg[:32] = -2 * np.pi
    sg[32:64] = 2 * np.pi
    put("sgn2pi", sg[:, None])
    put("eps", np.full((128, 1), EPS, np.float32))
    put("zero", np.zeros((128, 1), np.float32))
    put("quart", np.full((128, 1), 0.25, np.float32))

    p = np.arange(128)
    consts = np.zeros((128, 3, 128), np.float32)
    consts[:, 0, :] = np.eye(128, dtype=np.float32)
    consts[:, 1, :] = ((p[:, None] <= p[None, :]) & ((p[:, None] // 64) == (p[None, :] // 64))).astype(np.float32)
    consts[:, 2, :] = (p[:, None] <= p[None, :]).astype(np.float32)

    w_in1 = np.asarray(inp["w_in_odd"][0], np.float32)
    krope = w_in1[:, 1664:1728]
    w_in1x = np.concatenate([w_in1, krope[:, 32:64], krope[:, 0:32]], axis=1)
    wuq = np.asarray(inp["mla_w_uq"][0], np.float32).reshape(384, 4, 192)
    nope = wuq[:, :, 0:128].reshape(384, 512)
    rope = wuq[:, :, 128:192]
    ropep = np.concatenate([rope[:, :, 32:64], rope[:, :, 0:32]], axis=2)
    wuqx = np.concatenate([nope, rope.reshape(384, 256), ropep.reshape(384, 256)], axis=1)
    wukv = np.asarray(inp["mla_w_ukv"][0], np.float32).reshape(256, 4, 256)
    wukvx = np.concatenate([wukv[:, :, 0:128].reshape(256, 512), wukv[:, :, 128:256].reshape(256, 512)], axis=1)
    sgu_w = np.asarray(inp["sgu_w"][0], np.float32)
    sgu_wT = np.ascontiguousarray(sgu_w.transpose(2, 0, 1))
    return {
        "cols": cols, "consts": consts,
        "fin_norm": f(inp["final_norm"]).reshape(1, D),
        "sgu_bias": f(inp["sgu_b"][0]).reshape(1, 512),
        "sgu_wT": sgu_wT,
        "w_in0": f(inp["w_in_even"][0]), "w_out0": f(inp["w_out_even"][0]),
        "w_in1": f(w_in1x), "w_out1": f(inp["w_out_odd"][0]),
        "wuq": f(wuqx), "wukv": f(wukvx),
        "wg0": f(inp["ffn_gate"][0]), "wg1": f(inp["ffn_gate"][1]),
        "wu0": f(inp["ffn_up"][0]), "wu1": f(inp["ffn_up"][1]),
        "wd0": f(inp["ffn_down"][0]), "wd1": f(inp["ffn_down"][1]),
    }


_PROG = {}


def kernel(**inputs):
    x = np.asarray(inputs["x"], np.float32)
    pos = np.asarray(inputs["positions"], np.int32)
    shared = _host_shared(inputs)
    if "full" not in _PROG:
        _PROG["full"] = build_program()[0]
    nc = _PROG["full"]
    in_maps = []
    for c in range(NCORES):
        m = dict(shared)
        m["x"] = np.ascontiguousarray(x[2 * c:2 * c + 2].reshape(2 * SEQ, D))
        m["pos"] = np.ascontiguousarray(pos[2 * c:2 * c + 2])
        in_maps.append(m)
    res = run_bass_kernel_spmd(nc, in_maps, core_ids=list(range(NCORES)))
    out = np.stack([np.asarray(r["out"], np.float32).reshape(2, SEQ, D) for r in res.results], axis=0)
    return out.reshape(16, SEQ, D)
```
